# Optimizing a Trainium2 kernel written in Bass

```python
import math
import jax, jax.numpy as jnp
from jax import lax
import numpy as np

D_MODEL = 2048
BATCH = 4
SEQ = 2048
DEPTH = 1

MEM_LEN = 256
D_RNN = 1024
RNN_BLOCKS = 8
RNN_BLOCK = D_RNN // RNN_BLOCKS
CONV_W = 4
LRU_C = 8.0
SWA_HEADS = 16
SWA_KV_HEADS = 2
SWA_GROUP = SWA_HEADS // SWA_KV_HEADS
SWA_HD = 64
D_SWA = SWA_HEADS * SWA_HD
D_SWA_KV = SWA_KV_HEADS * SWA_HD
WINDOW = 128
BLOCK = WINDOW
MEM_HEADS = 4
MEM_HD = 256
D_MEM = MEM_HEADS * MEM_HD
REL_BUCKETS = 32
REL_MAX_DIST = 128
N_BRANCH = 3
EPS = 1e-6
NEG_INF = -1e30

IN_SPLITS = (D_RNN, D_RNN, D_SWA, D_SWA_KV, D_SWA_KV, D_SWA, D_MEM, D_MEM, N_BRANCH * D_MODEL)
D_IN = 2 * D_RNN + 2 * D_SWA + 2 * D_SWA_KV + 2 * D_MEM + N_BRANCH * D_MODEL

kernel_name = "hybrid_rglru_swa_sink_memxattn_gated"


def rmsnorm(x, g):
    xf = x.astype(jnp.float32)
    y = xf * lax.rsqrt(jnp.mean(xf * xf, axis=-1, keepdims=True) + EPS)
    return (y * g.astype(jnp.float32)).astype(x.dtype)


def rel_bucket(dist):
    n = jnp.maximum(dist, 0)
    max_exact = REL_BUCKETS // 2
    ratio = jnp.log(jnp.maximum(n, 1).astype(jnp.float32) / max_exact) / math.log(REL_MAX_DIST / max_exact)
    large = max_exact + (ratio * (REL_BUCKETS - max_exact)).astype(jnp.int32)
    large = jnp.minimum(large, REL_BUCKETS - 1)
    return jnp.where(n < max_exact, n, large)


def rglru_branch(xr, conv_w, conv_b, w_a, b_a, w_x, b_x, lam):
    B, S, _ = xr.shape
    xp = jnp.pad(xr, ((0, 0), (CONV_W - 1, 0), (0, 0)))
    conv = conv_b + sum(xp[:, CONV_W - 1 - k: CONV_W - 1 - k + S] * conv_w[k] for k in range(CONV_W))
    cb = conv.reshape(B, S, RNN_BLOCKS, RNN_BLOCK)
    gate_r = jax.nn.sigmoid((jnp.einsum('bsni,nij->bsnj', cb, w_a).reshape(B, S, D_RNN) + b_a).astype(jnp.float32))
    gate_i = jax.nn.sigmoid((jnp.einsum('bsni,nij->bsnj', cb, w_x).reshape(B, S, D_RNN) + b_x).astype(jnp.float32))
    log_a = -LRU_C * gate_r * jax.nn.softplus(-lam.astype(jnp.float32))
    a = jnp.exp(log_a)
    mult = jnp.sqrt(-jnp.expm1(2.0 * log_a))
    is_start = (jnp.arange(S) == 0)[None, :, None]
    mult = jnp.where(is_start, 1.0, mult)
    b = mult * gate_i * conv.astype(jnp.float32)

    def combine(e1, e2):
        a1, b1 = e1
        a2, b2 = e2
        return a1 * a2, a2 * b1 + b2

    _, h = lax.associative_scan(combine, (a, b), axis=1)
    return h.astype(xr.dtype)


def swa_branch(q, k, v, sinks, rel_bias):
    B, S, _ = q.shape
    nb = S // BLOCK
    q = q.reshape(B, nb, BLOCK, SWA_KV_HEADS, SWA_GROUP, SWA_HD)
    k = k.reshape(B, nb, BLOCK, SWA_KV_HEADS, SWA_HD)
    v = v.reshape(B, nb, BLOCK, SWA_KV_HEADS, SWA_HD)

    def with_prev(t):
        prev = jnp.concatenate([jnp.zeros_like(t[:, :1]), t[:, :-1]], axis=1)
        return jnp.concatenate([prev, t], axis=2)

    kk = with_prev(k)
    vv = with_prev(v)
    logits = jnp.einsum('bnqhgd,bnkhd->bnhgqk', q, kk).astype(jnp.float32) * (SWA_HD ** -0.5)

    qi = jnp.arange(BLOCK)[:, None]
    kj = jnp.arange(2 * BLOCK)[None, :]
    dist = qi + BLOCK - kj
    in_window = (dist >= 0) & (dist < WINDOW)
    key_abs = jnp.arange(nb)[:, None, None] * BLOCK + kj[None] - BLOCK
    valid = in_window[None] & (key_abs >= 0)

    bias = rel_bias.astype(jnp.float32)[rel_bucket(dist)]
    bias = jnp.transpose(bias, (2, 0, 1)).reshape(SWA_KV_HEADS, SWA_GROUP, BLOCK, 2 * BLOCK)
    logits = logits + bias[None, None]
    logits = jnp.where(valid[None, :, None, None], logits, NEG_INF)

    sink = sinks.astype(jnp.float32).reshape(SWA_KV_HEADS, SWA_GROUP)[None, None, :, :, None, None]
    m = jnp.maximum(jnp.max(logits, axis=-1, keepdims=True), sink)
    p = jnp.exp(logits - m)
    denom = jnp.sum(p, axis=-1, keepdims=True) + jnp.exp(sink - m)
    probs = (p / denom).astype(v.dtype)
    out = jnp.einsum('bnhgqk,bnkhd->bnqhgd', probs, vv)
    return out.reshape(B, S, D_SWA)


def mem_branch(q, mk, mv):
    B, S, _ = q.shape
    M = mk.shape[1]
    q = q.reshape(B, S, MEM_HEADS, MEM_HD)
    mk = mk.reshape(B, M, MEM_HEADS, MEM_HD)
    mv = mv.reshape(B, M, MEM_HEADS, MEM_HD)
    logits = jnp.einsum('bshd,bmhd->bhsm', q, mk).astype(jnp.float32) * (MEM_HD ** -0.5)
    probs = jax.nn.softmax(logits, axis=-1).astype(mv.dtype)
    out = jnp.einsum('bhsm,bmhd->bshd', probs, mv)
    return out.reshape(B, S, D_MEM)


def setup_inputs(seed: int = 0) -> dict:
    key = jax.random.key(seed)
    ks = jax.random.split(key, 24)
    f32 = jnp.float32
    nrm = lambda k, shape, s: jax.random.normal(k, shape, f32) * s
    L = DEPTH
    u = jax.random.uniform(ks[12], (L, D_RNN), f32, 0.9, 0.999)
    a0 = u ** (1.0 / LRU_C)
    lru_lambda = jnp.log(a0) - jnp.log1p(-a0)
    return {
        "x": nrm(ks[0], (BATCH, SEQ, D_MODEL), 1.0),
        "mem": nrm(ks[1], (BATCH, MEM_LEN, D_MODEL), 1.0),
        "pre_norm_g": 1.0 + nrm(ks[2], (L, D_MODEL), 0.02),
        "post_norm_g": 1.0 + nrm(ks[3], (L, D_MODEL), 0.02),
        "mem_norm_g": 1.0 + nrm(ks[4], (L, D_MODEL), 0.02),
        "w_in": nrm(ks[5], (L, D_MODEL, D_IN), D_MODEL ** -0.5),
        "conv_w": nrm(ks[6], (L, CONV_W, D_RNN), CONV_W ** -0.5),
        "conv_b": nrm(ks[7], (L, D_RNN), 0.02),
        "w_rg_a": nrm(ks[8], (L, RNN_BLOCKS, RNN_BLOCK, RNN_BLOCK), RNN_BLOCK ** -0.5),
        "b_rg_a": nrm(ks[9], (L, D_RNN), 0.02),
        "w_rg_x": nrm(ks[10], (L, RNN_BLOCKS, RNN_BLOCK, RNN_BLOCK), RNN_BLOCK ** -0.5),
        "b_rg_x": nrm(ks[11], (L, D_RNN), 0.02),
        "lru_lambda": lru_lambda,
        "swa_sinks": nrm(ks[13], (L, SWA_HEADS), 0.5),
        "rel_bias": nrm(ks[14], (REL_BUCKETS, SWA_HEADS), 0.5),
        "w_mem_kv": nrm(ks[15], (L, D_MODEL, 2 * D_MEM), D_MODEL ** -0.5),
        "w_br_rg": nrm(ks[16], (L, D_RNN, D_MODEL), D_RNN ** -0.5),
        "w_br_swa": nrm(ks[17], (L, D_SWA, D_MODEL), D_SWA ** -0.5),
        "w_br_mem": nrm(ks[18], (L, D_MEM, D_MODEL), D_MEM ** -0.5),
        "w_out": nrm(ks[19], (L, D_MODEL, D_MODEL), D_MODEL ** -0.5),
    }


def reference(x, mem, pre_norm_g, post_norm_g, mem_norm_g, w_in, conv_w, conv_b, w_rg_a, b_rg_a,
              w_rg_x, b_rg_x, lru_lambda, swa_sinks, rel_bias, w_mem_kv, w_br_rg, w_br_swa,
              w_br_mem, w_out):
    B, S, D = x.shape
    split_at = np.cumsum(IN_SPLITS)[:-1].tolist()
    for l in range(DEPTH):
        h = rmsnorm(x, pre_norm_g[l])
        proj = jnp.einsum('bsd,de->bse', h, w_in[l])
        (xr, g_rg, q_s, k_s, v_s, g_swa, q_m, g_mem, gate_logits) = jnp.split(proj, split_at, axis=-1)

        y_rg = rglru_branch(xr, conv_w[l], conv_b[l], w_rg_a[l], b_rg_a[l], w_rg_x[l], b_rg_x[l],
                            lru_lambda[l]) * jax.nn.silu(g_rg)
        y_swa = swa_branch(q_s, k_s, v_s, swa_sinks[l], rel_bias) * jax.nn.silu(g_swa)
        memn = rmsnorm(mem, mem_norm_g[l])
        mkv = jnp.einsum('bmd,de->bme', memn, w_mem_kv[l])
        mk, mv = jnp.split(mkv, 2, axis=-1)
        y_mem = mem_branch(q_m, mk, mv) * jax.nn.silu(g_mem)

        gates = jax.nn.sigmoid(gate_logits.astype(jnp.float32)).astype(x.dtype).reshape(B, S, N_BRANCH, D)
        merged = (gates[:, :, 0] * jnp.einsum('bsr,rd->bsd', y_rg, w_br_rg[l])
                  + gates[:, :, 1] * jnp.einsum('bsr,rd->bsd', y_swa, w_br_swa[l])
                  + gates[:, :, 2] * jnp.einsum('bsr,rd->bsd', y_mem, w_br_mem[l]))
        out = jnp.einsum('bsd,de->bse', merged, w_out[l])
        x = x + rmsnorm(out, post_norm_g[l])
    return x
```

```python
import os
from contextlib import ExitStack

import numpy as np
import concourse.bass as bass
import concourse.mybir as mybir
from concourse.bass_utils import run_bass_kernel_spmd

F32 = mybir.dt.float32
BF16 = mybir.dt.bfloat16
U8 = mybir.dt.uint8
AF = mybir.ActivationFunctionType
ALU = mybir.AluOpType

D = 2048
T = 1024
KC = 16
D_IN = 12544
EPS = 1e-6
NEG = -30000.0

C_XR, C_GRG, C_QS, C_K, C_V, C_GSWA, C_QM, C_GMEM, C_GATE = 0, 1024, 2048, 3072, 3200, 3328, 4352, 5376, 6400

PP_PREG, PP_MEMG, PP_CW, PP_CB, PP_BA, PP_BX, PP_LAM, PP_SINK, PP_FLAG, PP_OMF = 0, 16, 32, 64, 72, 80, 88, 96, 104, 105
NPP = 128
DR_CLAM, DR_ESINK, DR_HL, DR_XRT, DR_SS, DR_TMP = 0, 8, 16, 24, 48, 96
DR_HBA, DR_HBX, DR_HCL, DR_NEGF = 66, 74, 82, 90

ENGS = ("pe", "act", "dve", "pool", "sp")


class Hd:
    __slots__ = ("eng", "idx", "dma", "signal", "sem", "val")

    def __init__(self, eng, idx, dma):
        self.eng, self.idx, self.dma = eng, idx, dma
        self.signal = False
        self.sem = None
        self.val = 0


class V:
    __slots__ = ("ap", "sp", "lo", "hi")

    def __init__(self, ap, sp, lo, hi):
        self.ap, self.sp, self.lo, self.hi = ap, sp, lo, hi

    def __getitem__(self, k):
        return V(self.ap[k], self.sp, self.lo, self.hi)


class Sched:
    def __init__(self):
        self.ops = {e: [] for e in ENGS}
        self.iv = {"sb": [], "ps": []}
        self.final = []

    def _split(self, sp, lo, hi):
        ivs = self.iv[sp]
        out = []
        new = []
        cur = lo
        for it in ivs:
            a, b = it[0], it[1]
            if b <= lo or a >= hi:
                new.append(it)
                continue
            if a < lo:
                new.append([a, lo, it[2], dict(it[3]), list(it[4])])
                a = lo
            if b > hi:
                new.append([hi, b, it[2], dict(it[3]), list(it[4])])
                b = hi
            mid = [a, b, it[2], it[3], it[4]]
            new.append(mid)
            out.append(mid)
        out.sort(key=lambda x: x[0])
        gaps = []
        for it in out:
            if it[0] > cur:
                gaps.append([cur, it[0], None, {}, []])
            cur = max(cur, it[1])
        if cur < hi:
            gaps.append([cur, hi, None, {}, []])
        new.extend(gaps)
        out.extend(gaps)
        new.sort(key=lambda x: x[0])
        self.iv[sp] = new
        return out

    def op(self, eng, fn, reads=(), writes=(), dma=False, final=False):
        h = Hd(eng, len(self.ops[eng]), dma)
        deps = {}

        def add(d):
            if d is None:
                return
            if d.eng == "pe" and eng == "pe" and not d.dma:
                return
            deps[id(d)] = d

        touched_r = []
        touched_w = []
        for v in reads:
            for it in self._split(v.sp, v.lo, v.hi):
                add(it[2])
                touched_r.append(it)
        for v in writes:
            for it in self._split(v.sp, v.lo, v.hi):
                add(it[2])
                for r in it[3].values():
                    add(r)
                for r in it[4]:
                    add(r)
                touched_w.append(it)
        for it in touched_r:
            if dma:
                it[4].append(h)
            else:
                it[3][eng] = h
        for it in touched_w:
            it[2] = h
            it[3] = {}
            it[4] = []
        dl = [d for d in deps.values() if d is not h]
        for d in dl:
            d.signal = True
        if final:
            h.signal = True
            self.final.append(h)
        self.ops[eng].append((fn, dl, h))
        return h

    def emit(self, nc):
        R = 8
        with ExitStack() as st:
            sems = {e: st.enter_context(nc.semaphore("sem_" + e)) for e in ENGS}
            rings = {e: [st.enter_context(nc.semaphore("ring_%s_%d" % (e, i))) for i in range(R)]
                     for e in ("sp", "pool", "act")}
            ring_wait = {}
            for e in ENGS:
                cnt = 0
                nd = 0
                for (fn, dl, h) in self.ops[e]:
                    if h.dma:
                        h.sem = rings[e][nd % R]
                        h.val = 16 * (nd // R + 1)
                        if nd >= R:
                            ring_wait[id(h)] = (rings[e][nd % R], 16 * (nd // R))
                        nd += 1
                    elif h.signal:
                        cnt += 1
                        h.sem = sems[e]
                        h.val = cnt
            block = st.enter_context(nc.Block())

            def run(e, engobj):
                waited = {}

                def wait(sem, val):
                    k = id(sem)
                    if waited.get(k, 0) < val:
                        engobj.wait_ge(sem, val)
                        waited[k] = val

                for (fn, dl, h) in self.ops[e]:
                    for d in dl:
                        wait(d.sem, d.val)
                    if id(h) in ring_wait:
                        wait(*ring_wait[id(h)])
                    ins = fn(engobj)
                    if h.dma:
                        ins.then_inc(h.sem, 16)
                    elif h.signal:
                        ins.then_inc(h.sem, 1)
                if e == "sp":
                    for h in self.final:
                        wait(h.sem, h.val)

            @block.sync
            def _(eng):
                run("sp", eng)

            @block.scalar
            def _(eng):
                run("act", eng)

            @block.vector
            def _(eng):
                run("dve", eng)

            @block.gpsimd
            def _(eng):
                run("pool", eng)

            @block.tensor
            def _(eng):
                run("pe", eng)


ARENA_BYTES = 204 * 1024


def build_program(dbg=None):
    dbg = dbg or []
    nc = bass.Bass("TRN2", target_bir_lowering=False)
    S = Sched()

    def din(name, shape):
        return nc.dram_tensor(name, list(shape), F32, kind="ExternalInput").ap()

    xo = din("xo", [T, D])
    xp = din("xp", [T, D])
    memx = din("memx", [256, D])
    w_in = din("w_in", [D, D_IN])
    w_mkv = din("w_mkv", [D, 2048])
    w_br = din("w_br", [3, 1024, D])
    w_out = din("w_out", [D, D])
    w_rg = din("w_rg", [128, 2 * 8 * 128])
    pp_d = din("pp", [128, NPP])
    postg_d = din("postg", [128, D])
    biasg_d = din("biasg", [128, 16 * 256])
    maskc_d = din("maskc", [128, 256])
    ident_d = din("ident", [128, 128])
    out_d = nc.dram_tensor("out", [T, D], F32, kind="ExternalOutput").ap()
    dbg_d = {}
    for (name, shape, dt_) in dbg:
        dbg_d[name] = nc.dram_tensor("dbg_" + name, list(shape), BF16 if dt_ == "bf16" else F32,
                                     kind="ExternalOutput").ap()

    with ExitStack() as st:
        arena = st.enter_context(nc.sbuf_tensor("arena", [128, ARENA_BYTES], U8))
        psum = st.enter_context(nc.psum_tensor("psum", [128, 4096], F32))

        def sb(off, dtype, shape):
            esz = 4 if dtype == F32 else 2
            n = 1
            for s in shape[1:]:
                n *= s
            nb = n * esz
            assert off % 4 == 0 and off + nb <= ARENA_BYTES, (off, nb)
            ap = arena[:, off:off + nb].bitcast(dtype)
            if len(shape) == 3:
                ap = ap.rearrange("p (a b) -> p a b", a=shape[1])
            elif len(shape) == 4:
                ap = ap.rearrange("p (a b c) -> p a b c", a=shape[1], b=shape[2])
            elif len(shape) == 5:
                ap = ap.rearrange("p (a b c d) -> p a b c d", a=shape[1], b=shape[2], c=shape[3])
            return V(ap, "sb", off, off + nb)

        def bank(b, nb=1):
            return V(psum[:, b * 512:(b + nb) * 512], "ps", b * 2048, (b + nb) * 2048)

        def bank_bf(b, nb=1):
            return V(psum[:, b * 512:(b + nb) * 512].bitcast(BF16), "ps", b * 2048, (b + nb) * 2048)

        K1 = 1024
        IDENT = sb(0, BF16, [128, 128])
        OZ = sb(256, BF16, [128, 128])
        ZO = sb(512, BF16, [128, 128])
        ONES = sb(768, BF16, [128, 128])
        PP = sb(1 * K1, F32, [128, NPP])
        DER = sb(1 * K1 + 512, F32, [128, 128])
        WRG = sb(2 * K1, BF16, [128, 2, 8, 128])
        HT_OWN = sb(6 * K1, BF16, [128, KC, T])
        YB = [sb((38 + 16 * i) * K1, BF16, [128, 8, T]) for i in range(3)]
        HT_PRE = sb(54 * K1, BF16, [128, KC, T])
        WS = [sb((86 + 16 * i) * K1, BF16, [128, KC, 512]) for i in range(3)]
        KTD = sb(134 * K1, BF16, [128, 2, 1152])
        VV = sb(134 * K1 + 4608, BF16, [128, 9, 2, 2, 128])
        SCR = 134 * K1 + 4608 + 9216
        SCR_END = ARENA_BYTES

        def pp(col, n=1):
            return PP[:, col:col + n]

        def der(col, n=1):
            return V(DER.ap[:, col:col + n], "sb", DER.lo + 4 * col, DER.lo + 4 * (col + n))

        S.op("pool", lambda e: e.dma_start(IDENT.ap, ident_d), writes=[IDENT], dma=True)
        S.op("sp", lambda e: e.dma_start(PP.ap, pp_d), writes=[PP], dma=True)
        S.op("pool", lambda e: e.dma_start(WRG.ap.rearrange("p t n j -> p (t n j)"), w_rg), writes=[WRG], dma=True)
        S.op("pool", lambda e: e.memset(ONES.ap, 1.0), writes=[ONES])
        S.op("pool", lambda e: e.memset(OZ.ap, 0.0), writes=[OZ])
        S.op("pool", lambda e: e.memset(OZ.ap[:, 0:64], 1.0), writes=[OZ])
        S.op("pool", lambda e: e.memset(ZO.ap, 0.0), writes=[ZO])
        S.op("pool", lambda e: e.memset(ZO.ap[:, 64:128], 1.0), writes=[ZO])
        S.op("pool", lambda e: e.memset(VV.ap.rearrange("p a b c d -> p (a b c d)"), 0.0), writes=[VV])
        S.op("dve", lambda e: e.memset(DER.ap, 0.0), writes=[DER])
        S.op("act", lambda e: e.activation(der(DR_TMP, 8).ap, pp(PP_LAM, 8).ap, AF.Exp, scale=-1.0),
             reads=[PP], writes=[der(DR_TMP, 8)])
        S.op("act", lambda e: e.activation(der(DR_TMP + 8, 8).ap, der(DR_TMP, 8).ap, AF.Ln, bias=1.0),
             reads=[der(DR_TMP, 8)], writes=[der(DR_TMP + 8, 8)])
        S.op("dve", lambda e: e.tensor_scalar(der(DR_CLAM, 8).ap, der(DR_TMP + 8, 8).ap, -8.0, None, ALU.mult),
             reads=[der(DR_TMP + 8, 8)], writes=[der(DR_CLAM, 8)])
        S.op("act", lambda e: e.activation(der(DR_ESINK, 8).ap, pp(PP_SINK, 8).ap, AF.Exp),
             reads=[PP], writes=[der(DR_ESINK, 8)])

        def mm(out, lhsT, rhs, start, stop, reads, writes):
            return S.op("pe", lambda e: e.matmul(out, lhsT, rhs, start=start, stop=stop),
                        reads=reads, writes=writes)

        def norm_tiles(*a):
            for _ in norm_tiles_gen(*a):
                pass

        def norm_stages(src_d, g_col, HT, XS, XN, JUNK, TPB, ss_col, tile_map=None):
            tmap = tile_map or (lambda t: (src_d, t, HT))

            def stage0(gt):
                src_d, tt, _ = tmap(gt)
                xs = XS[gt % len(XS)]
                S.op("sp", lambda e: e.dma_start(xs.ap, src_d[tt * 128:(tt + 1) * 128, :]), writes=[xs], dma=True)

            def stage1(gt, with_load=True):
                if with_load:
                    stage0(gt)
                xs, xn = XS[gt % len(XS)], XN[gt % len(XN)]
                ssc = ss_col + 3 * (gt % 3)
                S.op("act", lambda e: e.activation(JUNK.ap, xs.ap, AF.Square, accum_out=der(ssc).ap),
                     reads=[xs], writes=[JUNK, der(ssc)])
                S.op("act", lambda e: e.activation(der(ssc + 1).ap, der(ssc).ap, AF.Ln, scale=1.0 / D, bias=EPS),
                     reads=[der(ssc)], writes=[der(ssc + 1)])
                S.op("act", lambda e: e.activation(der(ssc + 2).ap, der(ssc + 1).ap, AF.Exp, scale=-0.5),
                     reads=[der(ssc + 1)], writes=[der(ssc + 2)])
                S.op("act", lambda e: e.activation(xn.ap[:, 0:1024], xs.ap[:, 0:1024], AF.Copy, scale=der(ssc + 2).ap),
                     reads=[xs, der(ssc + 2)], writes=[xn])
                S.op("dve", lambda e: e.tensor_scalar(xn.ap[:, 1024:2048], xs.ap[:, 1024:2048], der(ssc + 2).ap, None,
                                                      ALU.mult),
                     reads=[xs, der(ssc + 2)], writes=[xn])

            def stage2(gt):
                _, tt, HT = tmap(gt)
                xn = XN[gt % len(XN)]
                tp = TPB[gt % len(TPB)]
                for kc in range(KC):
                    S.op("pe", lambda e, kc=kc: e.transpose(tp.ap[:, kc * 128:(kc + 1) * 128],
                                                            xn.ap[:, kc * 128:(kc + 1) * 128], IDENT.ap),
                         reads=[xn, IDENT], writes=[tp])
                S.op("dve", lambda e: e.tensor_tensor(
                    HT.ap[:, :, tt * 128:(tt + 1) * 128],
                    tp.ap.rearrange("p (a b) -> p a b", a=KC),
                    pp(g_col, KC).ap.unsqueeze(2).to_broadcast([128, KC, 128]), ALU.mult),
                    reads=[tp, PP], writes=[HT])

            stage1.load = stage0
            return stage1, stage2

        def norm_tiles_gen(src_d, ntiles, g_col, HT, XS, XN, JUNK, TPB, ss_col, tile_map=None):
            stage1, stage2 = norm_stages(src_d, g_col, HT, XS, XN, JUNK, TPB, ss_col, tile_map)
            skew = 1 if len(XN) >= 3 else 0
            for tt in range(min(skew, ntiles)):
                stage1(tt)
            for tt in range(ntiles):
                if tt + skew < ntiles:
                    stage1(tt + skew)
                stage2(tt)
                yield

        pj_ctr = [0]

        def proj(part, col, HT, t0, n, nk=KC):
            bk = bank(pj_ctr[0] % 2)
            pj_ctr[0] += 1
            for kc in range(nk):
                mm(bk.ap[:, 0:n], part.ap[:, kc, col:col + 128], HT.ap[:, kc, t0:t0 + n],
                   kc == 0, kc == nk - 1, [part, HT], [bk])
            return bk

        def dump(name, v):
            if name in dbg_d:
                S.op("sp", lambda e: e.dma_start(dbg_d[name], v.ap), reads=[v], dma=True, final=True)

        def wload(dst, src, nk=KC):
            S.op("pool", lambda e: e.dma_start(dst.ap, src.rearrange("(kc p) n -> p kc n", p=128)),
                 writes=[dst], dma=True)

        def make_load(specs, offs=None):
            def load(base):
                parts = []
                off = 0
                for si, src in enumerate(specs):
                    n = src.shape[1]
                    nk = src.shape[0] // 128
                    if offs is not None:
                        base, off = 0, offs[si]
                    v = sb(base + off, BF16, [128, nk, n])
                    wload(v, src)
                    parts.append(v)
                    off += nk * n * 2
                return parts
            return load

        tasks = []
        WS_LO = WS[0].lo

        o = SCR
        XS = [sb(o + 8 * K1 * i, F32, [128, D]) for i in range(3)]
        XN = [sb(o + (24 + 4 * i) * K1, BF16, [128, D]) for i in range(3)]
        JUNK = sb(o + 36 * K1, BF16, [128, D])
        WKV = sb(o + 40 * K1, BF16, [128, KC, 256])
        WKD = sb(o + 48 * K1, BF16, [128, KC, 2, 128])
        assert o + 56 * K1 <= SCR_END
        TPB = [bank_bf(2, 2), bank_bf(4, 2), bank_bf(6, 2)]

        wload(WKV, w_in[:, C_K:C_K + 256])
        def wkd_copies():
            for kvh in range(2):
                for half in range(2):
                    S.op("dve", lambda e, kvh=kvh, half=half: e.tensor_copy(
                        WKD.ap[:, :, kvh, half * 64:(half + 1) * 64], WKV.ap[:, :, kvh * 64:(kvh + 1) * 64]),
                        reads=[WKV], writes=[WKD])

        def k_group(kvh, HT, t0, n, c0):
            bk = bank(pj_ctr[0] % 2)
            pj_ctr[0] += 1
            for kc in range(KC):
                mm(bk.ap[:, 0:n], WKD.ap[:, kc, kvh, :], HT.ap[:, kc, t0:t0 + n], kc == 0, kc == KC - 1,
                   [WKD, HT], [bk])
            S.op("act", lambda e: e.copy(KTD.ap[:, kvh, c0:c0 + n], bk.ap[:, 0:n]), reads=[bk], writes=[KTD])

        def v_group(b):
            HT, t0 = (HT_PRE, 896) if b == 0 else (HT_OWN, (b - 1) * 128)
            bk = bank(pj_ctr[0] % 2)
            pj_ctr[0] += 1
            for kc in range(KC):
                mm(bk.ap[:, 0:128], HT.ap[:, kc, t0:t0 + 128], WKV.ap[:, kc, 128:256], kc == 0, kc == KC - 1,
                   [WKV, HT], [bk])
            for var in range(2):
                S.op("dve", lambda e, var=var: e.tensor_copy(
                    VV.ap[:, b, :, var, var * 64:(var + 1) * 64],
                    bk.ap[:, 0:128].rearrange("p (k d) -> p k d", k=2)),
                    reads=[bk], writes=[VV])

        kv_own = [lambda kvh=kvh, t0=t0, c0=c0: k_group(kvh, HT_OWN, t0, 512, c0)
                  for kvh in range(2) for (t0, c0) in ((0, 128), (512, 640))]
        kv_own += [lambda b=b: v_group(b) for b in range(1, 9)]
        tmap16 = lambda gt: (xo, gt, HT_OWN) if gt < 8 else (xp, gt - 8, HT_PRE)
        for gt, _ in enumerate(norm_tiles_gen(None, 16, PP_PREG, None, XS, XN, JUNK, TPB, DR_SS, tile_map=tmap16)):
            if gt == 7:
                wkd_copies()
            if gt >= 8:
                for _ in range(2 if gt - 8 < 4 else 1):
                    kv_own.pop(0)()
        assert not kv_own
        dump("ht_own", V(HT_OWN.ap.rearrange("p a b -> p (a b)"), "sb", HT_OWN.lo, HT_OWN.hi))
        for kvh in range(2):
            k_group(kvh, HT_PRE, 896, 128, 0)
        v_group(0)

        HBA, HBX, HCL = der(DR_HBA, 8), der(DR_HBX, 8), der(DR_HCL, 8)
        S.op("dve", lambda e: e.tensor_scalar(HBA.ap, pp(PP_BA, 8).ap, 0.5, None, ALU.mult), reads=[PP], writes=[HBA])
        S.op("dve", lambda e: e.tensor_scalar(HBX.ap, pp(PP_BX, 8).ap, 0.5, None, ALU.mult), reads=[PP], writes=[HBX])
        S.op("dve", lambda e: e.tensor_scalar(HCL.ap, der(DR_CLAM, 8).ap, 0.5, None, ALU.mult),
             reads=[der(DR_CLAM, 8)], writes=[HCL])
        RSZ = 4224 + 6 * 4096 - 2048
        rsets = []
        for i in range(2):
            o = SCR + i * RSZ
            rsets.append(dict(
                XR=sb(o, F32, [128, 1027]),
                XRH=V(None, "sb", o, o + 12), XRD=V(None, "sb", o + 12, o + 4 * 1027),
                CV=sb(o + 4224, F32, [128, T]),
                GA=sb(o + 4224 + 4096, F32, [128, T]),
                GI=sb(o + 4224 + 2 * 4096, F32, [128, T]),
                MU=sb(o + 4224 + 3 * 4096, F32, [128, T]),
                RS=sb(o + 4224 + 4 * 4096, F32, [128, T]),
                CVB=sb(o + 4224 + 5 * 4096, BF16, [128, T]),
            ))
        GB = [(bank(2), bank(3)), (bank(4), bank(5))]
        gb_ctr = [0]

        def rnn_info(q):
            own = q >= 8
            return own, q % 8, (HT_OWN if own else HT_PRE)

        def rnn_front_proj(q, xpart, xcol):
            own, c, HT = rnn_info(q)
            XR, XRH, XRD = rsets[q % 2]["XR"], rsets[q % 2]["XRH"], rsets[q % 2]["XRD"]
            xrt = der(DR_XRT + 3 * c, 3)
            if own:
                S.op("dve", lambda e: e.tensor_copy(XR.ap[:, 0:3], xrt.ap), reads=[xrt], writes=[XRH])
            else:
                S.op("dve", lambda e: e.memset(XR.ap[:, 0:3], 0.0), writes=[XRH])
            for nt in range(2):
                bk = proj(xpart, xcol, HT, nt * 512, 512)
                S.op("act", lambda e, bk=bk, nt=nt: e.copy(XR.ap[:, 3 + nt * 512:3 + (nt + 1) * 512], bk.ap),
                     reads=[bk], writes=[XRD])

        def rnn_conv(q):
            own, c, HT = rnn_info(q)
            rs = rsets[q % 2]
            XR, CV, CVB = rs["XR"], rs["CV"], rs["CVB"]
            cw = lambda k: pp(PP_CW + c * 4 + k).ap
            xrt = der(DR_XRT + 3 * c, 3)
            S.op("dve", lambda e: e.tensor_scalar(CV.ap, XR.ap[:, 3:3 + T], cw(0), pp(PP_CB + c).ap, ALU.mult, ALU.add),
                 reads=[XR, PP], writes=[CV])
            for k in range(1, 4):
                S.op("dve", lambda e, k=k: e.scalar_tensor_tensor(CV.ap, XR.ap[:, 3 - k:3 - k + T], cw(k), CV.ap,
                                                                  ALU.mult, ALU.add),
                     reads=[XR, CV, PP], writes=[CV])
            if not own:
                S.op("dve", lambda e: e.tensor_scalar(xrt.ap, XR.ap[:, T:T + 3], pp(PP_FLAG).ap, None, ALU.mult),
                     reads=[XR, PP], writes=[xrt])

        def rnn_cast(q):
            rs = rsets[q % 2]
            CV, CVB = rs["CV"], rs["CVB"]
            S.op("act", lambda e: e.copy(CVB.ap, CV.ap), reads=[CV], writes=[CVB])

        def rnn_step(q, gpart, gcol, nxt_q, nxt_part, nxt_col):
            own, c, HT = rnn_info(q)
            rs = rsets[q % 2]
            CV, GA, GI, MU, RS, CVB = (rs[k] for k in ("CV", "GA", "GI", "MU", "RS", "CVB"))
            hl = der(DR_HL + c)
            hcl = der(DR_HCL + c)
            if nxt_q is not None:
                rnn_front_proj(nxt_q, nxt_part, nxt_col)
            if own:
                for nt in range(2):
                    bk = proj(gpart, gcol, HT, nt * 512, 512)
                    S.op("act", lambda e, bk=bk, nt=nt: e.activation(RS.ap[:, nt * 512:(nt + 1) * 512], bk.ap, AF.Tanh,
                                                                   scale=0.5),
                         reads=[bk], writes=[RS])
                    S.op("dve", lambda e, bk=bk, nt=nt: e.scalar_tensor_tensor(
                        RS.ap[:, nt * 512:(nt + 1) * 512], RS.ap[:, nt * 512:(nt + 1) * 512], 1.0, bk.ap, ALU.add, ALU.mult),
                        reads=[RS, bk], writes=[RS])
            for nt in range(2):
                ba, bx = GB[gb_ctr[0] % 2]
                gb_ctr[0] += 1
                mm(ba.ap, WRG.ap[:, 0, c, :], CVB.ap[:, nt * 512:(nt + 1) * 512], True, True, [WRG, CVB], [ba])
                mm(bx.ap, WRG.ap[:, 1, c, :], CVB.ap[:, nt * 512:(nt + 1) * 512], True, True, [WRG, CVB], [bx])
                S.op("act", lambda e, ba=ba, nt=nt: e.activation(GA.ap[:, nt * 512:(nt + 1) * 512], ba.ap, AF.Tanh,
                                                               bias=der(DR_HBA + c).ap, scale=0.5),
                     reads=[ba, HBA], writes=[GA])
                S.op("act", lambda e, bx=bx, nt=nt: e.activation(GI.ap[:, nt * 512:(nt + 1) * 512], bx.ap, AF.Tanh,
                                                               bias=der(DR_HBX + c).ap, scale=0.5),
                     reads=[bx, HBX], writes=[GI])
            S.op("act", lambda e: e.activation(GA.ap, GA.ap, AF.Exp, scale=hcl.ap, bias=hcl.ap),
                 reads=[GA, hcl], writes=[GA])
            S.op("dve", lambda e: e.scalar_tensor_tensor(GI.ap, GI.ap, 1.0, CV.ap, ALU.add, ALU.mult),
                 reads=[GI, CV], writes=[GI])
            S.op("act", lambda e: e.activation(MU.ap, GA.ap, AF.Square), reads=[GA], writes=[MU])
            if q + 1 < 16:
                rnn_cast(q + 1)
            S.op("act", lambda e: e.activation(MU.ap, MU.ap, AF.Sqrt, scale=-1.0, bias=1.0), reads=[MU], writes=[MU])
            if nxt_q is not None:
                rnn_conv(nxt_q)
            if own:
                S.op("dve", lambda e: e.tensor_scalar(MU.ap[:, 0:1], MU.ap[:, 0:1], pp(PP_FLAG).ap, pp(PP_OMF).ap,
                                                      ALU.mult, ALU.add),
                     reads=[MU, PP], writes=[MU])
            else:
                S.op("dve", lambda e: e.memset(MU.ap[:, 0:1], 1.0), reads=[MU], writes=[MU])
            S.op("dve", lambda e: e.scalar_tensor_tensor(GI.ap, GI.ap, 0.25 if own else 0.5, MU.ap, ALU.mult, ALU.mult),
                 reads=[GI, MU], writes=[GI])
            init = hl.ap if own else 0.0
            S.op("dve", lambda e: e.tensor_tensor_scan(MU.ap, GA.ap, GI.ap, init, ALU.mult, ALU.add),
                 reads=[GA, GI, hl], writes=[MU])
            if own:
                S.op("pool", lambda e: e.tensor_tensor(YB[0].ap[:, c, :], MU.ap, RS.ap, ALU.mult),
                     reads=[MU, RS], writes=[YB[0]])
            else:
                S.op("dve", lambda e: e.tensor_scalar(hl.ap, MU.ap[:, T - 1:T], pp(PP_FLAG).ap, 0.5, ALU.mult, ALU.mult),
                     reads=[MU, PP], writes=[hl])

        def ws_base(k):
            return WS_LO + (k % 3) * 16 * K1

        rnn_tasks = []
        for own in (False, True):
            for g in range(4):
                specs = [w_in[:, C_XR + g * 256:C_XR + (g + 1) * 256]]
                if own:
                    specs.append(w_in[:, C_GRG + g * 256:C_GRG + (g + 1) * 256])
                rnn_tasks.append((own, g, specs))

        early_hooks = []

        def rnn_comp(parts, nxt, ti):
            if ti == 5:
                for hk in early_hooks:
                    hk()
            if ti == 0:
                for q in (0, 1):
                    rnn_front_proj(q, parts[0], q * 128)
                    rnn_conv(q)
                rnn_cast(0)
            for q in (2 * ti, 2 * ti + 1):
                nq = q + 2 if q + 2 < 16 else None
                rnn_step(q, parts[-1], (q % 2) * 128, nq, (nxt[0] if nq is not None else None), (q % 2) * 128)
            if ti == 7:
                dump("y_rg", V(YB[0].ap.rearrange("p a b -> p (a b)"), "sb", YB[0].lo, YB[0].hi))

        for ti, (own, g, specs) in enumerate(rnn_tasks):
            tasks.append((ws_base, make_load(specs), (lambda parts, nxt, ti=ti: rnn_comp(parts, nxt, ti))))

        o = SCR
        BIAST = sb(o, F32, [128, 16, 256]); o += 16 * K1
        QZs = [[sb(o + 4 * K1 * (2 * i + hh_), BF16, [128, 2, T]) for hh_ in range(2)] for i in range(2)]; o += 16 * K1
        QZh = [sb(YB[1].lo + 8 * K1 + 4 * K1 * hh_, BF16, [128, 2, T]) for hh_ in range(2)]
        QZb = [QZh, QZs[0], QZs[1], QZs[0]]
        SGs = [sb(o, F32, [128, 2, T]), sb(YB[2].lo, F32, [128, 2, T])]; o += 8 * K1
        EIN = [sb(o + 2 * K1 * i, F32, [128, 512]) for i in range(4)]; o += 8 * K1
        MASKC = sb(EIN[0].lo, F32, [128, 256])
        EX = [sb(o + K1 * i, BF16, [128, 2, 2, 128]) for i in range(4)]; o += 4 * K1
        RD2 = [sb(o + K1 * i, F32, [128, 256]) for i in range(2)]; o += 2 * K1
        TN2 = [sb(o + K1 * i, F32, [128, 256]) for i in range(2)]; o += 2 * K1
        assert o <= SCR_END, o
        LB = [bank(2 + i) for i in range(4)]
        ND = [bank(6), bank(7)]
        lb_ctr = [0]

        def swa_setup():
            S.op("dve", lambda e: e.tensor_scalar(der(DR_NEGF).ap, pp(PP_FLAG).ap, -1.0, -NEG, ALU.add, ALU.mult),
                 reads=[PP], writes=[der(DR_NEGF)])
            swa_zero([1, 2])
            S.op("sp", lambda e: e.dma_start(BIAST.ap.rearrange("p a b -> p (a b)"), biasg_d), writes=[BIAST], dma=True)
            S.op("sp", lambda e: e.dma_start(MASKC.ap, maskc_d), writes=[MASKC], dma=True)
            S.op("dve", lambda e: e.tensor_tensor(BIAST.ap, BIAST.ap, MASKC.ap.unsqueeze(1).to_broadcast([128, 16, 256]),
                                                  ALU.add),
                 reads=[BIAST, MASKC], writes=[BIAST])

        def swa_zero(idx):
            for i in idx:
                for hh_ in range(2):
                    qz = QZb[i][hh_]
                    S.op("dve", lambda e, qz=qz: e.memset(qz.ap.rearrange("p a b -> p (a b)"), 0.0), writes=[qz])

        early_hooks.append(lambda: swa_zero([0]))

        def swa_proj_gen(parts, g):
            QZ = QZb[g]
            SG = SGs[(g + 1) % 2]
            for j in range(4):
                for nt in range(2):
                    bk = proj(parts[j // 2], (j % 2) * 128, HT_OWN, nt * 512, 512)
                    if j < 2:
                        for hh_ in range(2):
                            S.op("act", lambda e, bk=bk, j=j, nt=nt, hh_=hh_: e.copy(
                                QZ[hh_].ap[hh_ * 64:(hh_ + 1) * 64, j, nt * 512:(nt + 1) * 512],
                                bk.ap[hh_ * 64:(hh_ + 1) * 64, :]),
                                reads=[bk], writes=[QZ[hh_]])
                    else:
                        sgv = SG.ap[:, j - 2, nt * 512:(nt + 1) * 512]
                        S.op("act", lambda e, bk=bk, sgv=sgv: e.activation(sgv, bk.ap, AF.Exp, scale=-1.0),
                             reads=[bk], writes=[SG])
                        S.op("act", lambda e, sgv=sgv: e.activation(sgv, sgv, AF.Ln, bias=1.0), reads=[SG], writes=[SG])
                        S.op("act", lambda e, sgv=sgv: e.activation(sgv, sgv, AF.Exp, scale=-1.0), reads=[SG], writes=[SG])
                        S.op("dve", lambda e, bk=bk, sgv=sgv: e.scalar_tensor_tensor(sgv, sgv, 2.0, bk.ap, ALU.mult, ALU.mult),
                             reads=[SG, bk], writes=[SG])
                    yield

        def swa_attn(g, gen):
            QZ = QZb[g]
            SG = SGs[(g + 1) % 2]
            blocks = [(cl, i) for cl in range(2) for i in range(8)]
            npair = len(blocks) // 2

            def qk(b):
                cl, i = blocks[b]
                c = 2 * g + cl
                kvh = c // 4
                lb = LB[b % 4]
                for hh in range(2):
                    for j in range(2):
                        mm(lb.ap[:, (hh * 2 + j) * 128:(hh * 2 + j + 1) * 128],
                           KTD.ap[:, kvh, (i + j) * 128:(i + j + 1) * 128],
                           QZ[hh].ap[:, cl, i * 128:(i + 1) * 128],
                           True, True, [KTD, QZ[hh]], [lb])

            def bias(b):
                cl, i = blocks[b]
                c = 2 * g + cl
                lb, ein = LB[b % 4], EIN[b % 4]
                S.op("dve", lambda e: e.scalar_tensor_tensor(
                    ein.ap, lb.ap, 0.125, BIAST.ap[:, 2 * c:2 * c + 2, :].rearrange("p a b -> p (a b)"),
                    ALU.mult, ALU.add),
                    reads=[lb, BIAST], writes=[ein])
                if i == 0:
                    v4 = ein.ap.rearrange("p (a b c) -> p a b c", a=2, b=2)[:, :, 0, :]
                    S.op("dve", lambda e: e.tensor_scalar(v4, v4, der(DR_NEGF).ap, None, ALU.add),
                         reads=[ein, der(DR_NEGF)], writes=[ein])

            def expo(b):
                ein, ex = EIN[b % 4], EX[b % 4]
                S.op("act", lambda e: e.activation(ex.ap.rearrange("p a b c -> p (a b c)"), ein.ap, AF.Exp),
                     reads=[ein], writes=[ex])

            def pv(b):
                cl, i = blocks[b]
                c = 2 * g + cl
                kvh = c // 4
                ex = EX[b % 4]
                ndb = ND[(b // 2) % 2]
                s_ = b % 2
                n_ = 0
                for hh in range(2):
                    for j in range(2):
                        mm(ndb.ap[:, s_ * 128:(s_ + 1) * 128], VV.ap[:, i + j, kvh, hh, :], ex.ap[:, hh, j, :],
                           n_ == 0, n_ == 3, [VV, ex], [ndb])
                        n_ += 1
                n_ = 0
                for hh in range(2):
                    for j in range(2):
                        mm(ndb.ap[:, 256 + s_ * 128:256 + (s_ + 1) * 128], (OZ if hh == 0 else ZO).ap, ex.ap[:, hh, j, :],
                           n_ == 0, n_ == 3, [OZ, ZO, ex], [ndb])
                        n_ += 1

            def n_act(p):
                cl, i = blocks[2 * p]
                esk = der(DR_ESINK + 2 * g + cl)
                ndb, rd = ND[p % 2], RD2[p % 2]
                S.op("act", lambda e: e.activation(rd.ap, ndb.ap[:, 256:512], AF.Ln, bias=esk.ap),
                     reads=[ndb, esk], writes=[rd])
                S.op("act", lambda e: e.activation(rd.ap, rd.ap, AF.Exp, scale=-1.0), reads=[rd], writes=[rd])

            def n_dve(p):
                cl, i = blocks[2 * p]
                c = 2 * g + cl
                ndb, rd, tn = ND[p % 2], RD2[p % 2], TN2[p % 2]
                t0 = i * 128
                S.op("dve", lambda e: e.tensor_tensor(tn.ap, ndb.ap[:, 0:256], rd.ap, ALU.mult),
                     reads=[ndb, rd], writes=[tn])
                ych = V(None, "sb", YB[1].lo + c * 2 * K1, YB[1].lo + (c + 1) * 2 * K1)
                S.op("dve", lambda e: e.scalar_tensor_tensor(
                    YB[1].ap[:, c, t0:t0 + 256], tn.ap, 0.5, SG.ap[:, cl, t0:t0 + 256], ALU.mult, ALU.mult),
                    reads=[tn, SG], writes=[ych])

            qk(0); qk(1); bias(0); bias(1)
            ngen = 0
            for k in range(npair + 2):
                if k + 1 < npair:
                    qk(2 * k + 2); qk(2 * k + 3)
                    bias(2 * k + 2); bias(2 * k + 3)
                if 0 <= k - 2 < npair:
                    n_dve(k - 2)
                if k < npair:
                    expo(2 * k); expo(2 * k + 1)
                if 0 <= k - 1 < npair:
                    n_act(k - 1)
                if gen is not None and ngen < 8 and k < 8:
                    ngen += 1
                    next(gen, None)
                if k < npair:
                    pv(2 * k); pv(2 * k + 1)

        swa_gen = [None]

        swa_hooks = {}

        def swa_comp(parts, nxt, g):
            if g == 0:
                swa_setup()
                for _ in swa_proj_gen(parts, 0):
                    pass
            gen = swa_proj_gen(nxt, g + 1) if g < 3 else swa_hooks["g3_gen"]()
            swa_attn(g, gen)
            if gen is not None:
                for _ in gen:
                    pass
            if g == 2:
                swa_hooks["after_g2"]()
            if g == 3:
                dump("y_swa", V(YB[1].ap.rearrange("p a b -> p (a b)"), "sb", YB[1].lo, YB[1].hi))

        for g in range(4):
            specs = [w_in[:, C_QS + g * 256:C_QS + (g + 1) * 256], w_in[:, C_GSWA + g * 256:C_GSWA + (g + 1) * 256]]
            tasks.append((ws_base, make_load(specs), (lambda parts, nxt, g=g: swa_comp(parts, nxt, g))))

        o = SCR
        MEMT = sb(YB[2].lo, BF16, [128, KC, 256])
        MXN = [sb(YB[2].lo + 8 * K1, BF16, [128, D])]
        MJ = sb(YB[2].lo + 12 * K1, BF16, [128, D])
        MXS = [sb(QZs[1][0].lo, F32, [128, D])]
        MKT = sb(o, BF16, [128, 8, 256]); o += 4 * K1
        MV = sb(o, BF16, [128, 2, 1024]); o += 4 * K1
        o2 = o
        mn1, mn2 = norm_stages(memx, PP_MEMG, MEMT, MXS, MXN, MJ, [bank_bf(0, 2)], DR_SS + 9)

        def mem_norm_gen():
            yield
            yield
            mn1(0, with_load=False); mn1.load(1)
            yield
            yield
            mn2(0)
            yield
            mn1(1, with_load=False)
            yield
            yield
            mn2(1)
            yield
        swa_hooks["after_g2"] = lambda: mn1.load(0)
        swa_hooks["g3_gen"] = mem_norm_gen
        QMb = [sb(o2 + 4 * K1 * i, BF16, [128, 2, T]) for i in range(2)]; o2 += 8 * K1
        SGMb = [sb(o2 + 8 * K1 * i, F32, [128, 2, T]) for i in range(2)]; o2 += 16 * K1
        EXM = [sb(o2 + 2 * K1 * i, BF16, [128, 2, 512]) for i in range(2)]; o2 += 4 * K1
        RDMb = [sb(o2 + 2 * K1 * i, F32, [128, 512]) for i in range(2)]; o2 += 4 * K1
        TNM = [sb(o2 + 2 * K1 * i, F32, [128, 512]) for i in range(2)]; o2 += 4 * K1
        assert o2 <= SCR_END, o2
        LMB = [bank(2), bank(3)]
        NMB = [bank(4), bank(5)]
        DMB = bank(6)
        ex_ctr = [0]

        def mkv_group(parts, gg):
            part = parts[0]
            if gg < 2:
                for j in range(4):
                    bk = proj(part, j * 128, MEMT, 0, 256)
                    S.op("act", lambda e, bk=bk, j=j: e.copy(MKT.ap[:, gg * 4 + j, :], bk.ap[:, 0:256]),
                         reads=[bk], writes=[MKT])
            else:
                for mc in range(2):
                    bk = bank(pj_ctr[0] % 2)
                    pj_ctr[0] += 1
                    for kc in range(KC):
                        mm(bk.ap, MEMT.ap[:, kc, mc * 128:(mc + 1) * 128], part.ap[:, kc, :], kc == 0, kc == KC - 1,
                           [MEMT, part], [bk])
                    S.op("act", lambda e, bk=bk, mc=mc: e.copy(MV.ap[:, mc, (gg - 2) * 512:(gg - 1) * 512], bk.ap),
                         reads=[bk], writes=[MV])

        def mem_proj_gen(parts, m):
            QM, SGM = QMb[m % 2], SGMb[m % 2]
            for j in range(4):
                for nt in range(2):
                    bk = proj(parts[j // 2], (j % 2) * 128, HT_OWN, nt * 512, 512)
                    if j < 2:
                        S.op("act", lambda e, bk=bk, j=j, nt=nt: e.copy(QM.ap[:, j, nt * 512:(nt + 1) * 512], bk.ap),
                             reads=[bk], writes=[QM])
                    else:
                        sgv = SGM.ap[:, j - 2, nt * 512:(nt + 1) * 512]
                        S.op("act", lambda e, bk=bk, sgv=sgv: e.activation(sgv, bk.ap, AF.Exp, scale=-1.0),
                             reads=[bk], writes=[SGM])
                        S.op("act", lambda e, sgv=sgv: e.activation(sgv, sgv, AF.Ln, bias=1.0), reads=[SGM], writes=[SGM])
                        S.op("act", lambda e, sgv=sgv: e.activation(sgv, sgv, AF.Exp, scale=-1.0), reads=[SGM], writes=[SGM])
                        S.op("dve", lambda e, bk=bk, sgv=sgv: e.scalar_tensor_tensor(sgv, sgv, 2.0, bk.ap, ALU.mult, ALU.mult),
                             reads=[SGM, bk], writes=[SGM])
                    yield

        def mem_group(parts, nxt, m):
            QM, SGM = QMb[m % 2], SGMb[m % 2]
            if m == 0:
                for _ in mem_proj_gen(parts, 0):
                    pass
            gen = mem_proj_gen(nxt, m + 1) if m < 3 else None

            def adv(n):
                if gen is not None:
                    for _ in range(n):
                        next(gen, None)

            exms = [EXM[0], EXM[1]]

            def qk(nt):
                for mc in range(2):
                    lm = LMB[mc]
                    for dc in range(2):
                        mm(lm.ap, MKT.ap[:, 2 * m + dc, mc * 128:(mc + 1) * 128], QM.ap[:, dc, nt * 512:(nt + 1) * 512],
                           dc == 0, dc == 1, [MKT, QM], [lm])

            def expo(nt):
                exm = exms[nt]
                for mc in range(2):
                    lm = LMB[mc]
                    S.op("act", lambda e, lm=lm, mc=mc: e.activation(exm.ap[:, mc, :], lm.ap, AF.Exp, scale=1.0 / 16),
                         reads=[lm], writes=[exm])

            def pv(nt):
                exm = exms[nt]
                for dc in range(2):
                    for mc in range(2):
                        mm(NMB[dc].ap, MV.ap[:, mc, (2 * m + dc) * 128:(2 * m + dc + 1) * 128], exm.ap[:, mc, :],
                           mc == 0, mc == 1, [MV, exm], [NMB[dc]])
                for mc in range(2):
                    mm(DMB.ap, ONES.ap, exm.ap[:, mc, :], mc == 0, mc == 1, [ONES, exm], [DMB])

            def n_act(nt):
                RDM = RDMb[nt]
                S.op("act", lambda e: e.activation(RDM.ap, DMB.ap, AF.Ln), reads=[DMB], writes=[RDM])
                S.op("act", lambda e: e.activation(RDM.ap, RDM.ap, AF.Exp, scale=-1.0), reads=[RDM], writes=[RDM])

            def n_dve(nt):
                for dc in range(2):
                    tn = TNM[dc]
                    S.op("dve", lambda e, dc=dc, tn=tn, RDM=RDMb[nt]: e.tensor_tensor(tn.ap, NMB[dc].ap, RDM.ap, ALU.mult),
                         reads=[NMB[dc], RDMb[nt]], writes=[tn])
                    S.op("dve", lambda e, dc=dc, tn=tn: e.scalar_tensor_tensor(
                        YB[2].ap[:, 2 * m + dc, nt * 512:(nt + 1) * 512], tn.ap, 0.5,
                        SGM.ap[:, dc, nt * 512:(nt + 1) * 512], ALU.mult, ALU.mult),
                        reads=[tn, SGM], writes=[YB[2]])

            qk(0); adv(2); expo(0); pv(0)
            qk(1); adv(2); expo(1); n_act(0); n_dve(0)
            adv(2); pv(1); adv(2); n_act(1); n_dve(1)
            if gen is not None:
                for _ in gen:
                    pass
            if m == 3:
                dump("y_mem", V(YB[2].ap.rearrange("p a b -> p (a b)"), "sb", YB[2].lo, YB[2].hi))

        for gg in range(4):
            tasks.append((ws_base, make_load([w_mkv[:, gg * 512:(gg + 1) * 512]]), (lambda parts, nxt, gg=gg: mkv_group(parts, gg))))
        for m in range(4):
            specs = [w_in[:, C_QM + m * 256:C_QM + (m + 1) * 256], w_in[:, C_GMEM + m * 256:C_GMEM + (m + 1) * 256]]
            tasks.append((ws_base, make_load(specs), (lambda parts, nxt, m=m: mem_group(parts, nxt, m))))

        MERGED = sb(ARENA_BYTES - 32 * K1, BF16, [128, KC, T])
        o = 156 * K1
        SIG = [sb(o + 2 * K1 * i, F32, [128, 512]) for i in range(3)]; o += 6 * K1
        ACC = [sb(o + 2 * K1 * i, F32, [128, 512]) for i in range(2)]; o += 4 * K1
        TT = [sb(o + 2 * K1 * i, F32, [128, 512]) for i in range(2)]; o += 4 * K1
        assert o <= MERGED.lo
        WO = [sb(140 * K1, BF16, [128, KC, 512]), sb(6 * K1, BF16, [128, KC, 512]),
              sb(22 * K1, BF16, [128, KC, 512]), sb(102 * K1, BF16, [128, KC, 512])]
        bk_ctr = [0]
        sg_ctr = [0]

        def s5_base(k):
            return WS_LO + (k % 3) * 18 * K1

        def s5_compute(parts, dc):
            G5 = parts[0:3]
            B5 = parts[3:6]
            if dc == 1:
                wload(WO[0], w_out[:, 0:512])
            for nt in range(2):
                acc = ACC[(dc * 2 + nt) % 2]
                for i in range(3):
                    ba = bank(bk_ctr[0] % 8)
                    bb = bank((bk_ctr[0] + 1) % 8)
                    bk_ctr[0] += 2
                    sg = SIG[sg_ctr[0] % 3]
                    tt_ = TT[sg_ctr[0] % 2]
                    sg_ctr[0] += 1
                    for kc in range(KC):
                        mm(ba.ap, G5[i].ap[:, kc, :], HT_OWN.ap[:, kc, nt * 512:(nt + 1) * 512], kc == 0, kc == KC - 1,
                           [G5[i], HT_OWN], [ba])
                    for kc in range(8):
                        mm(bb.ap, B5[i].ap[:, kc, :], YB[i].ap[:, kc, nt * 512:(nt + 1) * 512], kc == 0, kc == 7,
                           [B5[i], YB[i]], [bb])
                    S.op("act", lambda e, ba=ba, sg=sg: e.activation(sg.ap, ba.ap, AF.Sigmoid), reads=[ba], writes=[sg])
                    if i == 0:
                        S.op("dve", lambda e, sg=sg, bb=bb, acc=acc: e.tensor_tensor(acc.ap, sg.ap, bb.ap, ALU.mult),
                             reads=[sg, bb], writes=[acc])
                    else:
                        S.op("dve", lambda e, sg=sg, bb=bb, tt_=tt_: e.tensor_tensor(tt_.ap, sg.ap, bb.ap, ALU.mult),
                             reads=[sg, bb], writes=[tt_])
                        if i == 1:
                            S.op("dve", lambda e, acc=acc, tt_=tt_: e.tensor_tensor(acc.ap, acc.ap, tt_.ap, ALU.add),
                                 reads=[acc, tt_], writes=[acc])
                        else:
                            S.op("dve", lambda e, acc=acc, tt_=tt_, nt=nt: e.tensor_tensor(
                                MERGED.ap[:, dc, nt * 512:(nt + 1) * 512], acc.ap, tt_.ap, ALU.add),
                                reads=[acc, tt_], writes=[MERGED])

        for dc in range(16):
            specs = [w_in[:, C_GATE + i * 2048 + dc * 128:C_GATE + i * 2048 + (dc + 1) * 128] for i in range(3)]
            specs += [w_br[i, :, dc * 128:(dc + 1) * 128] for i in range(3)]
            if dc == 0:
                offs0 = [KTD.lo + 4 * K1 * i for i in range(3)] + [SCR + 50 * K1 + 2 * K1 * i for i in range(3)]
                tasks.append((s5_base, make_load(specs, offs0), (lambda parts, nxt, dc=dc: s5_compute(parts, dc)), True))
            else:
                tasks.append((s5_base, make_load(specs), (lambda parts, nxt, dc=dc: s5_compute(parts, dc))))

        loaded = []
        ring_k = {}
        for i in range(len(tasks)):
            while len(loaded) <= i or (len(loaded) < min(len(tasks), i + 3) and
                                       (tasks[len(loaded)][0] is tasks[i][0] or len(tasks[len(loaded)]) > 3)):
                k = len(loaded)
                basefn, load = tasks[k][0], tasks[k][1]
                if len(tasks[k]) > 3:
                    loaded.append(load(0))
                    continue
                rk = ring_k.get(basefn, 0)
                ring_k[basefn] = rk + 1
                loaded.append(load(basefn(rk)))
            tasks[i][2](loaded[i], loaded[i + 1] if i + 1 < len(loaded) else None)
        dump("merged", V(MERGED.ap.rearrange("p a b -> p (a b)"), "sb", MERGED.lo, MERGED.hi))

        for cg in range(1, 4):
            wload(WO[cg], w_out[:, cg * 512:(cg + 1) * 512])
        OUTU = [sb((38 + 8 * i) * K1, F32, [128, D]) for i in range(8)]
        POSTG = sb(118 * K1, F32, [128, D])
        XRS = [sb(126 * K1, F32, [128, D]), sb(156 * K1, F32, [128, D])]
        OJ = sb(134 * K1, BF16, [128, D])
        S.op("sp", lambda e: e.dma_start(POSTG.ap, postg_d), writes=[POSTG], dma=True)
        def s6_mm(cg, tt):
            bk = bank(bk_ctr[0] % 8)
            bk_ctr[0] += 1
            for kc in range(KC):
                mm(bk.ap, MERGED.ap[:, kc, tt * 128:(tt + 1) * 128], WO[cg].ap[:, kc, :],
                   kc == 0, kc == KC - 1, [MERGED, WO[cg]], [bk])
            S.op("act", lambda e: e.copy(OUTU[tt].ap[:, cg * 512:(cg + 1) * 512], bk.ap),
                 reads=[bk], writes=[OUTU[tt]])

        for cg in range(2):
            for tt in range(8):
                s6_mm(cg, tt)
        for tt in range(8):
            ou, xrs = OUTU[tt], XRS[tt % 2]
            ssc = 112 + 3 * (tt % 2)
            s6_mm(2, tt)
            s6_mm(3, tt)
            S.op("act", lambda e, ou=ou, ssc=ssc: e.activation(OJ.ap, ou.ap, AF.Square, accum_out=der(ssc).ap),
                 reads=[ou], writes=[OJ, der(ssc)])
            S.op("act", lambda e, ssc=ssc: e.activation(der(ssc + 1).ap, der(ssc).ap, AF.Ln, scale=1.0 / D, bias=EPS),
                 reads=[der(ssc)], writes=[der(ssc + 1)])
            S.op("act", lambda e, ssc=ssc: e.activation(der(ssc + 2).ap, der(ssc + 1).ap, AF.Exp, scale=-0.5),
                 reads=[der(ssc + 1)], writes=[der(ssc + 2)])
            if tt == 0:
                S.op("sp", lambda e: e.dma_start(XRS[0].ap, xo[7 * 128:8 * 128, :]), writes=[XRS[0]], dma=True)
            npiece = 2 if tt == 7 else 1
            w_ = D // npiece
            for pc in range(npiece):
                ouv = V(ou.ap[:, pc * w_:(pc + 1) * w_], "sb", ou.lo + 4 * pc * w_, ou.lo + 4 * (pc + 1) * w_)
                pgv = POSTG.ap[:, pc * w_:(pc + 1) * w_]
                S.op("dve", lambda e, ouv=ouv, pgv=pgv, ssc=ssc: e.scalar_tensor_tensor(
                    ouv.ap, ouv.ap, der(ssc + 2).ap, pgv, ALU.mult, ALU.mult),
                    reads=[ouv, der(ssc + 2), POSTG], writes=[ouv])
                if tt == 7:
                    xrv = XRS[0].ap[:, pc * w_:(pc + 1) * w_]
                    S.op("dve", lambda e, ouv=ouv, xrv=xrv: e.tensor_tensor(ouv.ap, ouv.ap, xrv, ALU.add),
                         reads=[ouv, XRS[0]], writes=[ouv])
                else:
                    S.op("pool", lambda e, ouv=ouv, tt=tt, pc=pc, w_=w_: e.dma_start(
                        ouv.ap, xo[tt * 128:(tt + 1) * 128, pc * w_:(pc + 1) * w_], accum_op=ALU.add),
                        reads=[ouv], writes=[ouv], dma=True)
                S.op("sp", lambda e, ouv=ouv, tt=tt, pc=pc, w_=w_: e.dma_start(
                    out_d[tt * 128:(tt + 1) * 128, pc * w_:(pc + 1) * w_], ouv.ap),
                    reads=[ouv], dma=True, final=True)

        S.emit(nc)
    return nc


def _rel_bucket_np(dist):
    n = np.maximum(dist, 0)
    max_exact = 16
    ratio = np.log(np.maximum(n, 1).astype(np.float32) / np.float32(max_exact)) / np.float32(np.log(128 / max_exact))
    large = max_exact + (ratio.astype(np.float32) * np.float32(32 - max_exact)).astype(np.int32)
    large = np.minimum(large, 31)
    return np.where(n < max_exact, n, large)


def _fm(v):
    return np.ascontiguousarray(v.reshape(-1, 128).T)


def make_in_maps(x, mem, pre_norm_g, post_norm_g, mem_norm_g, w_in, conv_w, conv_b, w_rg_a, b_rg_a,
                 w_rg_x, b_rg_x, lru_lambda, swa_sinks, rel_bias, w_mem_kv, w_br_rg, w_br_swa,
                 w_br_mem, w_out):
    f = np.float32
    x = np.asarray(x, f)
    mem = np.asarray(mem, f)
    w_in0 = np.ascontiguousarray(np.asarray(w_in, f)[0])
    w_mkv0 = np.ascontiguousarray(np.asarray(w_mem_kv, f)[0])
    w_br0 = np.ascontiguousarray(np.stack([np.asarray(w_br_rg, f)[0], np.asarray(w_br_swa, f)[0],
                                           np.asarray(w_br_mem, f)[0]]))
    w_out0 = np.ascontiguousarray(np.asarray(w_out, f)[0])
    wrg = np.stack([np.asarray(w_rg_a, f)[0], np.asarray(w_rg_x, f)[0]])
    wrg = np.ascontiguousarray(wrg.transpose(2, 0, 1, 3).reshape(128, 2 * 8 * 128))
    k = np.arange(128)[:, None, None]
    j = np.arange(2)[None, :, None]
    q = np.arange(128)[None, None, :]
    dist = q + 128 - (j * 128 + k)
    valid = (dist >= 0) & (dist < 128)
    bucket = _rel_bucket_np(dist)
    rb = np.asarray(rel_bias, f)
    biasg = rb[bucket]
    biasg = np.ascontiguousarray(biasg.transpose(0, 3, 1, 2).reshape(128, 16 * 256))
    maskc = np.ascontiguousarray(np.where(valid, f(0.0), f(NEG)).astype(f).reshape(128, 256))
    ident = np.eye(128, dtype=f)
    postg = np.ascontiguousarray(np.broadcast_to(np.asarray(post_norm_g, f)[0][None, :], (128, D)))
    sinks = np.asarray(swa_sinks, f)[0]
    sinkc = np.empty((128, 8), f)
    for c in range(8):
        sinkc[0:64, c] = sinks[2 * c]
        sinkc[64:128, c] = sinks[2 * c + 1]
    cw = np.asarray(conv_w, f)[0]
    in_maps = []
    for core in range(8):
        b, half = core // 2, core % 2
        pp = np.zeros((128, NPP), f)
        pp[:, PP_PREG:PP_PREG + 16] = _fm(np.asarray(pre_norm_g, f)[0])
        pp[:, PP_MEMG:PP_MEMG + 16] = _fm(np.asarray(mem_norm_g, f)[0])
        for kk in range(4):
            pp[:, PP_CW + kk:PP_CW + 32:4] = _fm(cw[kk])
        pp[:, PP_CB:PP_CB + 8] = _fm(np.asarray(conv_b, f)[0])
        pp[:, PP_BA:PP_BA + 8] = _fm(np.asarray(b_rg_a, f)[0])
        pp[:, PP_BX:PP_BX + 8] = _fm(np.asarray(b_rg_x, f)[0])
        pp[:, PP_LAM:PP_LAM + 8] = _fm(np.asarray(lru_lambda, f)[0])
        pp[:, PP_SINK:PP_SINK + 8] = sinkc
        pp[:, PP_FLAG] = f(half)
        pp[:, PP_OMF] = f(1 - half)
        xo = np.ascontiguousarray(x[b, half * T:(half + 1) * T])
        xpre = np.ascontiguousarray(x[b, 0:T]) if half == 1 else np.zeros((T, D), f)
        in_maps.append({
            "xo": xo, "xp": xpre, "memx": np.ascontiguousarray(mem[b]),
            "w_in": w_in0, "w_mkv": w_mkv0, "w_br": w_br0, "w_out": w_out0, "w_rg": wrg,
            "pp": pp, "postg": postg, "biasg": biasg, "maskc": maskc, "ident": ident,
        })
    return in_maps


_DBG = []


def kernel(**inputs):
    in_maps = make_in_maps(**inputs)
    nc = build_program(_DBG)
    res = run_bass_kernel_spmd(nc, in_maps, core_ids=list(range(8)))
    out = np.empty((4, 2048, D), np.float32)
    for core in range(8):
        b, half = core // 2, core % 2
        out[b, half * T:(half + 1) * T] = res.results[core]["out"]
    if _DBG:
        kernel.last_results = res.results
    return out
```

```python
import os
from contextlib import ExitStack

import numpy as np
import concourse.bass as bass
import concourse.mybir as mybir
from concourse.bass_utils import run_bass_kernel_spmd

F32 = mybir.dt.float32
BF16 = mybir.dt.bfloat16
U8 = mybir.dt.uint8
AF = mybir.ActivationFunctionType
ALU = mybir.AluOpType

D = 2048
T = 1024
KC = 16
D_IN = 12544
EPS = 1e-6
NEG = -30000.0

C_XR, C_GRG, C_QS, C_K, C_V, C_GSWA, C_QM, C_GMEM, C_GATE = 0, 1024, 2048, 3072, 3200, 3328, 4352, 5376, 6400

PP_PREG, PP_MEMG, PP_CW, PP_CB, PP_BA, PP_BX, PP_LAM, PP_SINK, PP_FLAG, PP_OMF = 0, 16, 32, 64, 72, 80, 88, 96, 104, 105
NPP = 128
DR_CLAM, DR_ESINK, DR_HL, DR_XRT, DR_SS, DR_TMP = 0, 8, 16, 24, 48, 96
DR_HBA, DR_HBX, DR_HCL, DR_NEGF = 66, 74, 82, 90

ENGS = ("pe", "act", "dve", "pool", "sp")


class Hd:
    __slots__ = ("eng", "idx", "dma", "signal", "sem", "val")

    def __init__(self, eng, idx, dma):
        self.eng, self.idx, self.dma = eng, idx, dma
        self.signal = False
        self.sem = None
        self.val = 0


class V:
    __slots__ = ("ap", "sp", "lo", "hi")

    def __init__(self, ap, sp, lo, hi):
        self.ap, self.sp, self.lo, self.hi = ap, sp, lo, hi

    def __getitem__(self, k):
        return V(self.ap[k], self.sp, self.lo, self.hi)


class Sched:
    def __init__(self):
        self.ops = {e: [] for e in ENGS}
        self.iv = {"sb": [], "ps": []}
        self.final = []

    def _split(self, sp, lo, hi):
        ivs = self.iv[sp]
        out = []
        new = []
        cur = lo
        for it in ivs:
            a, b = it[0], it[1]
            if b <= lo or a >= hi:
                new.append(it)
                continue
            if a < lo:
                new.append([a, lo, it[2], dict(it[3]), list(it[4])])
                a = lo
            if b > hi:
                new.append([hi, b, it[2], dict(it[3]), list(it[4])])
                b = hi
            mid = [a, b, it[2], it[3], it[4]]
            new.append(mid)
            out.append(mid)
        out.sort(key=lambda x: x[0])
        gaps = []
        for it in out:
            if it[0] > cur:
                gaps.append([cur, it[0], None, {}, []])
            cur = max(cur, it[1])
        if cur < hi:
            gaps.append([cur, hi, None, {}, []])
        new.extend(gaps)
        out.extend(gaps)
        new.sort(key=lambda x: x[0])
        self.iv[sp] = new
        return out

    def op(self, eng, fn, reads=(), writes=(), dma=False, final=False):
        h = Hd(eng, len(self.ops[eng]), dma)
        deps = {}

        def add(d):
            if d is None:
                return
            if d.eng == "pe" and eng == "pe" and not d.dma:
                return
            deps[id(d)] = d

        touched_r = []
        touched_w = []
        for v in reads:
            for it in self._split(v.sp, v.lo, v.hi):
                add(it[2])
                touched_r.append(it)
        for v in writes:
            for it in self._split(v.sp, v.lo, v.hi):
                add(it[2])
                for r in it[3].values():
                    add(r)
                for r in it[4]:
                    add(r)
                touched_w.append(it)
        for it in touched_r:
            if dma:
                it[4].append(h)
            else:
                it[3][eng] = h
        for it in touched_w:
            it[2] = h
            it[3] = {}
            it[4] = []
        dl = [d for d in deps.values() if d is not h]
        for d in dl:
            d.signal = True
        if final:
            h.signal = True
            self.final.append(h)
        self.ops[eng].append((fn, dl, h))
        return h

    def emit(self, nc):
        R = 8
        with ExitStack() as st:
            sems = {e: st.enter_context(nc.semaphore("sem_" + e)) for e in ENGS}
            rings = {e: [st.enter_context(nc.semaphore("ring_%s_%d" % (e, i))) for i in range(R)]
                     for e in ("sp", "pool", "act")}
            ring_wait = {}
            for e in ENGS:
                cnt = 0
                nd = 0
                for (fn, dl, h) in self.ops[e]:
                    if h.dma:
                        h.sem = rings[e][nd % R]
                        h.val = 16 * (nd // R + 1)
                        if nd >= R:
                            ring_wait[id(h)] = (rings[e][nd % R], 16 * (nd // R))
                        nd += 1
                    elif h.signal:
                        cnt += 1
                        h.sem = sems[e]
                        h.val = cnt
            block = st.enter_context(nc.Block())

            def run(e, engobj):
                waited = {}

                def wait(sem, val):
                    k = id(sem)
                    if waited.get(k, 0) < val:
                        engobj.wait_ge(sem, val)
                        waited[k] = val

                for (fn, dl, h) in self.ops[e]:
                    for d in dl:
                        wait(d.sem, d.val)
                    if id(h) in ring_wait:
                        wait(*ring_wait[id(h)])
                    ins = fn(engobj)
                    if h.dma:
                        ins.then_inc(h.sem, 16)
                    elif h.signal:
                        ins.then_inc(h.sem, 1)
                if e == "sp":
                    for h in self.final:
                        wait(h.sem, h.val)

            @block.sync
            def _(eng):
                run("sp", eng)

            @block.scalar
            def _(eng):
                run("act", eng)

            @block.vector
            def _(eng):
                run("dve", eng)

            @block.gpsimd
            def _(eng):
                run("pool", eng)

            @block.tensor
            def _(eng):
                run("pe", eng)


ARENA_BYTES = 204 * 1024


def build_program(dbg=None):
    dbg = dbg or []
    nc = bass.Bass("TRN2", target_bir_lowering=False)
    S = Sched()

    def din(name, shape):
        return nc.dram_tensor(name, list(shape), F32, kind="ExternalInput").ap()

    xo = din("xo", [T, D])
    xp = din("xp", [T, D])
    memx = din("memx", [256, D])
    w_in = din("w_in", [D, D_IN])
    w_mkv = din("w_mkv", [D, 2048])
    w_br = din("w_br", [3, 1024, D])
    w_out = din("w_out", [D, D])
    w_rg = din("w_rg", [128, 2 * 8 * 128])
    pp_d = din("pp", [128, NPP])
    postg_d = din("postg", [128, D])
    biasg_d = din("biasg", [128, 16 * 256])
    maskc_d = din("maskc", [128, 256])
    ident_d = din("ident", [128, 128])
    out_d = nc.dram_tensor("out", [T, D], F32, kind="ExternalOutput").ap()
    dbg_d = {}
    for (name, shape, dt_) in dbg:
        dbg_d[name] = nc.dram_tensor("dbg_" + name, list(shape), BF16 if dt_ == "bf16" else F32,
                                     kind="ExternalOutput").ap()

    with ExitStack() as st:
        arena = st.enter_context(nc.sbuf_tensor("arena", [128, ARENA_BYTES], U8))
        psum = st.enter_context(nc.psum_tensor("psum", [128, 4096], F32))

        def sb(off, dtype, shape):
            esz = 4 if dtype == F32 else 2
            n = 1
            for s in shape[1:]:
                n *= s
            nb = n * esz
            assert off % 4 == 0 and off + nb <= ARENA_BYTES, (off, nb)
            ap = arena[:, off:off + nb].bitcast(dtype)
            if len(shape) == 3:
                ap = ap.rearrange("p (a b) -> p a b", a=shape[1])
            elif len(shape) == 4:
                ap = ap.rearrange("p (a b c) -> p a b c", a=shape[1], b=shape[2])
            elif len(shape) == 5:
                ap = ap.rearrange("p (a b c d) -> p a b c d", a=shape[1], b=shape[2], c=shape[3])
            return V(ap, "sb", off, off + nb)

        def bank(b, nb=1):
            return V(psum[:, b * 512:(b + nb) * 512], "ps", b * 2048, (b + nb) * 2048)

        def bank_bf(b, nb=1):
            return V(psum[:, b * 512:(b + nb) * 512].bitcast(BF16), "ps", b * 2048, (b + nb) * 2048)

        K1 = 1024
        IDENT = sb(0, BF16, [128, 128])
        OZ = sb(256, BF16, [128, 128])
        ZO = sb(512, BF16, [128, 128])
        ONES = sb(768, BF16, [128, 128])
        PP = sb(1 * K1, F32, [128, NPP])
        DER = sb(1 * K1 + 512, F32, [128, 128])
        WRG = sb(2 * K1, BF16, [128, 2, 8, 128])
        HT_OWN = sb(6 * K1, BF16, [128, KC, T])
        YB = [sb((38 + 16 * i) * K1, BF16, [128, 8, T]) for i in range(3)]
        HT_PRE = sb(54 * K1, BF16, [128, KC, T])
        WS = [sb((86 + 16 * i) * K1, BF16, [128, KC, 512]) for i in range(3)]
        KTD = sb(134 * K1, BF16, [128, 2, 1152])
        VV = sb(134 * K1 + 4608, BF16, [128, 9, 2, 2, 128])
        SCR = 134 * K1 + 4608 + 9216
        SCR_END = ARENA_BYTES

        def pp(col, n=1):
            return PP[:, col:col + n]

        def der(col, n=1):
            return V(DER.ap[:, col:col + n], "sb", DER.lo + 4 * col, DER.lo + 4 * (col + n))

        S.op("pool", lambda e: e.dma_start(IDENT.ap, ident_d), writes=[IDENT], dma=True)
        S.op("sp", lambda e: e.dma_start(PP.ap, pp_d), writes=[PP], dma=True)
        S.op("pool", lambda e: e.dma_start(WRG.ap.rearrange("p t n j -> p (t n j)"), w_rg), writes=[WRG], dma=True)
        S.op("pool", lambda e: e.memset(ONES.ap, 1.0), writes=[ONES])
        S.op("pool", lambda e: e.memset(OZ.ap, 0.0), writes=[OZ])
        S.op("pool", lambda e: e.memset(OZ.ap[:, 0:64], 1.0), writes=[OZ])
        S.op("pool", lambda e: e.memset(ZO.ap, 0.0), writes=[ZO])
        S.op("pool", lambda e: e.memset(ZO.ap[:, 64:128], 1.0), writes=[ZO])
        S.op("pool", lambda e: e.memset(VV.ap.rearrange("p a b c d -> p (a b c d)"), 0.0), writes=[VV])
        S.op("dve", lambda e: e.memset(DER.ap, 0.0), writes=[DER])
        S.op("act", lambda e: e.activation(der(DR_TMP, 8).ap, pp(PP_LAM, 8).ap, AF.Exp, scale=-1.0),
             reads=[PP], writes=[der(DR_TMP, 8)])
        S.op("act", lambda e: e.activation(der(DR_TMP + 8, 8).ap, der(DR_TMP, 8).ap, AF.Ln, bias=1.0),
             reads=[der(DR_TMP, 8)], writes=[der(DR_TMP + 8, 8)])
        S.op("dve", lambda e: e.tensor_scalar(der(DR_CLAM, 8).ap, der(DR_TMP + 8, 8).ap, -8.0, None, ALU.mult),
             reads=[der(DR_TMP + 8, 8)], writes=[der(DR_CLAM, 8)])
        S.op("act", lambda e: e.activation(der(DR_ESINK, 8).ap, pp(PP_SINK, 8).ap, AF.Exp),
             reads=[PP], writes=[der(DR_ESINK, 8)])

        def mm(out, lhsT, rhs, start, stop, reads, writes):
            return S.op("pe", lambda e: e.matmul(out, lhsT, rhs, start=start, stop=stop),
                        reads=reads, writes=writes)

        def norm_tiles(*a):
            for _ in norm_tiles_gen(*a):
                pass

        def norm_stages(src_d, g_col, HT, XS, XN, JUNK, TPB, ss_col, tile_map=None):
            tmap = tile_map or (lambda t: (src_d, t, HT))

            def stage0(gt):
                src_d, tt, _ = tmap(gt)
                xs = XS[gt % len(XS)]
                S.op("sp", lambda e: e.dma_start(xs.ap, src_d[tt * 128:(tt + 1) * 128, :]), writes=[xs], dma=True)

            def stage1(gt, with_load=True):
                if with_load:
                    stage0(gt)
                xs, xn = XS[gt % len(XS)], XN[gt % len(XN)]
                ssc = ss_col + 3 * (gt % 3)
                S.op("act", lambda e: e.activation(JUNK.ap, xs.ap, AF.Square, accum_out=der(ssc).ap),
                     reads=[xs], writes=[JUNK, der(ssc)])
                S.op("act", lambda e: e.activation(der(ssc + 1).ap, der(ssc).ap, AF.Ln, scale=1.0 / D, bias=EPS),
                     reads=[der(ssc)], writes=[der(ssc + 1)])
                S.op("act", lambda e: e.activation(der(ssc + 2).ap, der(ssc + 1).ap, AF.Exp, scale=-0.5),
                     reads=[der(ssc + 1)], writes=[der(ssc + 2)])
                S.op("act", lambda e: e.activation(xn.ap[:, 0:1024], xs.ap[:, 0:1024], AF.Copy, scale=der(ssc + 2).ap),
                     reads=[xs, der(ssc + 2)], writes=[xn])
                S.op("dve", lambda e: e.tensor_scalar(xn.ap[:, 1024:2048], xs.ap[:, 1024:2048], der(ssc + 2).ap, None,
                                                      ALU.mult),
                     reads=[xs, der(ssc + 2)], writes=[xn])

            def stage2(gt):
                _, tt, HT = tmap(gt)
                xn = XN[gt % len(XN)]
                tp = TPB[gt % len(TPB)]
                for kc in range(KC):
                    S.op("pe", lambda e, kc=kc: e.transpose(tp.ap[:, kc * 128:(kc + 1) * 128],
                                                            xn.ap[:, kc * 128:(kc + 1) * 128], IDENT.ap),
                         reads=[xn, IDENT], writes=[tp])
                S.op("dve", lambda e: e.tensor_tensor(
                    HT.ap[:, :, tt * 128:(tt + 1) * 128],
                    tp.ap.rearrange("p (a b) -> p a b", a=KC),
                    pp(g_col, KC).ap.unsqueeze(2).to_broadcast([128, KC, 128]), ALU.mult),
                    reads=[tp, PP], writes=[HT])

            stage1.load = stage0
            return stage1, stage2

        def norm_tiles_gen(src_d, ntiles, g_col, HT, XS, XN, JUNK, TPB, ss_col, tile_map=None):
            stage1, stage2 = norm_stages(src_d, g_col, HT, XS, XN, JUNK, TPB, ss_col, tile_map)
            skew = 1 if len(XN) >= 3 else 0
            for tt in range(min(skew, ntiles)):
                stage1(tt)
            for tt in range(ntiles):
                if tt + skew < ntiles:
                    stage1(tt + skew)
                stage2(tt)
                yield

        pj_ctr = [0]

        def proj(part, col, HT, t0, n, nk=KC):
            bk = bank(pj_ctr[0] % 2)
            pj_ctr[0] += 1
            for kc in range(nk):
                mm(bk.ap[:, 0:n], part.ap[:, kc, col:col + 128], HT.ap[:, kc, t0:t0 + n],
                   kc == 0, kc == nk - 1, [part, HT], [bk])
            return bk

        def dump(name, v):
            if name in dbg_d:
                S.op("sp", lambda e: e.dma_start(dbg_d[name], v.ap), reads=[v], dma=True, final=True)

        def wload(dst, src, nk=KC):
            S.op("pool", lambda e: e.dma_start(dst.ap, src.rearrange("(kc p) n -> p kc n", p=128)),
                 writes=[dst], dma=True)

        def make_load(specs, offs=None):
            def load(base):
                parts = []
                off = 0
                for si, src in enumerate(specs):
                    n = src.shape[1]
                    nk = src.shape[0] // 128
                    if offs is not None:
                        base, off = 0, offs[si]
                    v = sb(base + off, BF16, [128, nk, n])
                    wload(v, src)
                    parts.append(v)
                    off += nk * n * 2
                return parts
            return load

        tasks = []
        WS_LO = WS[0].lo

        o = SCR
        XS = [sb(o + 8 * K1 * i, F32, [128, D]) for i in range(3)]
        XN = [sb(o + (24 + 4 * i) * K1, BF16, [128, D]) for i in range(3)]
        JUNK = sb(o + 36 * K1, BF16, [128, D])
        WKV = sb(o + 40 * K1, BF16, [128, KC, 256])
        WKD = sb(o + 48 * K1, BF16, [128, KC, 2, 128])
        assert o + 56 * K1 <= SCR_END
        TPB = [bank_bf(2, 2), bank_bf(4, 2), bank_bf(6, 2)]

        wload(WKV, w_in[:, C_K:C_K + 256])
        def wkd_copies():
            for kvh in range(2):
                for half in range(2):
                    S.op("dve", lambda e, kvh=kvh, half=half: e.tensor_copy(
                        WKD.ap[:, :, kvh, half * 64:(half + 1) * 64], WKV.ap[:, :, kvh * 64:(kvh + 1) * 64]),
                        reads=[WKV], writes=[WKD])

        def k_group(kvh, HT, t0, n, c0):
            bk = bank(pj_ctr[0] % 2)
            pj_ctr[0] += 1
            for kc in range(KC):
                mm(bk.ap[:, 0:n], WKD.ap[:, kc, kvh, :], HT.ap[:, kc, t0:t0 + n], kc == 0, kc == KC - 1,
                   [WKD, HT], [bk])
            S.op("act", lambda e: e.copy(KTD.ap[:, kvh, c0:c0 + n], bk.ap[:, 0:n]), reads=[bk], writes=[KTD])

        def v_group(b):
            HT, t0 = (HT_PRE, 896) if b == 0 else (HT_OWN, (b - 1) * 128)
            bk = bank(pj_ctr[0] % 2)
            pj_ctr[0] += 1
            for kc in range(KC):
                mm(bk.ap[:, 0:128], HT.ap[:, kc, t0:t0 + 128], WKV.ap[:, kc, 128:256], kc == 0, kc == KC - 1,
                   [WKV, HT], [bk])
            for var in range(2):
                S.op("dve", lambda e, var=var: e.tensor_copy(
                    VV.ap[:, b, :, var, var * 64:(var + 1) * 64],
                    bk.ap[:, 0:128].rearrange("p (k d) -> p k d", k=2)),
                    reads=[bk], writes=[VV])

        kv_own = [lambda kvh=kvh, t0=t0, c0=c0: k_group(kvh, HT_OWN, t0, 512, c0)
                  for kvh in range(2) for (t0, c0) in ((0, 128), (512, 640))]
        kv_own += [lambda b=b: v_group(b) for b in range(1, 9)]
        tmap16 = lambda gt: (xo, gt, HT_OWN) if gt < 8 else (xp, gt - 8, HT_PRE)
        for gt, _ in enumerate(norm_tiles_gen(None, 16, PP_PREG, None, XS, XN, JUNK, TPB, DR_SS, tile_map=tmap16)):
            if gt == 7:
                wkd_copies()
            if gt >= 8:
                for _ in range(2 if gt - 8 < 4 else 1):
                    kv_own.pop(0)()
        assert not kv_own
        dump("ht_own", V(HT_OWN.ap.rearrange("p a b -> p (a b)"), "sb", HT_OWN.lo, HT_OWN.hi))
        for kvh in range(2):
            k_group(kvh, HT_PRE, 896, 128, 0)
        v_group(0)

        HBA, HBX, HCL = der(DR_HBA, 8), der(DR_HBX, 8), der(DR_HCL, 8)
        S.op("dve", lambda e: e.tensor_scalar(HBA.ap, pp(PP_BA, 8).ap, 0.5, None, ALU.mult), reads=[PP], writes=[HBA])
        S.op("dve", lambda e: e.tensor_scalar(HBX.ap, pp(PP_BX, 8).ap, 0.5, None, ALU.mult), reads=[PP], writes=[HBX])
        S.op("dve", lambda e: e.tensor_scalar(HCL.ap, der(DR_CLAM, 8).ap, 0.5, None, ALU.mult),
             reads=[der(DR_CLAM, 8)], writes=[HCL])
        RSZ = 4224 + 6 * 4096 - 2048
        rsets = []
        for i in range(2):
            o = SCR + i * RSZ
            rsets.append(dict(
                XR=sb(o, F32, [128, 1027]),
                XRH=V(None, "sb", o, o + 12), XRD=V(None, "sb", o + 12, o + 4 * 1027),
                CV=sb(o + 4224, F32, [128, T]),
                GA=sb(o + 4224 + 4096, F32, [128, T]),
                GI=sb(o + 4224 + 2 * 4096, F32, [128, T]),
                MU=sb(o + 4224 + 3 * 4096, F32, [128, T]),
                RS=sb(o + 4224 + 4 * 4096, F32, [128, T]),
                CVB=sb(o + 4224 + 5 * 4096, BF16, [128, T]),
            ))
        GB = [(bank(2), bank(3)), (bank(4), bank(5))]
        gb_ctr = [0]

        def rnn_info(q):
            own = q >= 8
            return own, q % 8, (HT_OWN if own else HT_PRE)

        def rnn_front_proj(q, xpart, xcol):
            own, c, HT = rnn_info(q)
            XR, XRH, XRD = rsets[q % 2]["XR"], rsets[q % 2]["XRH"], rsets[q % 2]["XRD"]
            xrt = der(DR_XRT + 3 * c, 3)
            if own:
                S.op("dve", lambda e: e.tensor_copy(XR.ap[:, 0:3], xrt.ap), reads=[xrt], writes=[XRH])
            else:
                S.op("dve", lambda e: e.memset(XR.ap[:, 0:3], 0.0), writes=[XRH])
            for nt in range(2):
                bk = proj(xpart, xcol, HT, nt * 512, 512)
                S.op("act", lambda e, bk=bk, nt=nt: e.copy(XR.ap[:, 3 + nt * 512:3 + (nt + 1) * 512], bk.ap),
                     reads=[bk], writes=[XRD])

        def rnn_conv(q):
            own, c, HT = rnn_info(q)
            rs = rsets[q % 2]
            XR, CV, CVB = rs["XR"], rs["CV"], rs["CVB"]
            cw = lambda k: pp(PP_CW + c * 4 + k).ap
            xrt = der(DR_XRT + 3 * c, 3)
            S.op("dve", lambda e: e.tensor_scalar(CV.ap, XR.ap[:, 3:3 + T], cw(0), pp(PP_CB + c).ap, ALU.mult, ALU.add),
                 reads=[XR, PP], writes=[CV])
            for k in range(1, 4):
                S.op("dve", lambda e, k=k: e.scalar_tensor_tensor(CV.ap, XR.ap[:, 3 - k:3 - k + T], cw(k), CV.ap,
                                                                  ALU.mult, ALU.add),
                     reads=[XR, CV, PP], writes=[CV])
            if not own:
                S.op("dve", lambda e: e.tensor_scalar(xrt.ap, XR.ap[:, T:T + 3], pp(PP_FLAG).ap, None, ALU.mult),
                     reads=[XR, PP], writes=[xrt])

        def rnn_cast(q):
            rs = rsets[q % 2]
            CV, CVB = rs["CV"], rs["CVB"]
            S.op("act", lambda e: e.copy(CVB.ap, CV.ap), reads=[CV], writes=[CVB])

        def rnn_step(q, gpart, gcol, nxt_q, nxt_part, nxt_col):
            own, c, HT = rnn_info(q)
            rs = rsets[q % 2]
            CV, GA, GI, MU, RS, CVB = (rs[k] for k in ("CV", "GA", "GI", "MU", "RS", "CVB"))
            hl = der(DR_HL + c)
            hcl = der(DR_HCL + c)
            if nxt_q is not None:
                rnn_front_proj(nxt_q, nxt_part, nxt_col)
            if own:
                for nt in range(2):
                    bk = proj(gpart, gcol, HT, nt * 512, 512)
                    S.op("act", lambda e, bk=bk, nt=nt: e.activation(RS.ap[:, nt * 512:(nt + 1) * 512], bk.ap, AF.Tanh,
                                                                   scale=0.5),
                         reads=[bk], writes=[RS])
                    S.op("dve", lambda e, bk=bk, nt=nt: e.scalar_tensor_tensor(
                        RS.ap[:, nt * 512:(nt + 1) * 512], RS.ap[:, nt * 512:(nt + 1) * 512], 1.0, bk.ap, ALU.add, ALU.mult),
                        reads=[RS, bk], writes=[RS])
            for nt in range(2):
                ba, bx = GB[gb_ctr[0] % 2]
                gb_ctr[0] += 1
                mm(ba.ap, WRG.ap[:, 0, c, :], CVB.ap[:, nt * 512:(nt + 1) * 512], True, True, [WRG, CVB], [ba])
                mm(bx.ap, WRG.ap[:, 1, c, :], CVB.ap[:, nt * 512:(nt + 1) * 512], True, True, [WRG, CVB], [bx])
                S.op("act", lambda e, ba=ba, nt=nt: e.activation(GA.ap[:, nt * 512:(nt + 1) * 512], ba.ap, AF.Tanh,
                                                               bias=der(DR_HBA + c).ap, scale=0.5),
                     reads=[ba, HBA], writes=[GA])
                S.op("act", lambda e, bx=bx, nt=nt: e.activation(GI.ap[:, nt * 512:(nt + 1) * 512], bx.ap, AF.Tanh,
                                                               bias=der(DR_HBX + c).ap, scale=0.5),
                     reads=[bx, HBX], writes=[GI])
            S.op("act", lambda e: e.activation(GA.ap, GA.ap, AF.Exp, scale=hcl.ap, bias=hcl.ap),
                 reads=[GA, hcl], writes=[GA])
            S.op("dve", lambda e: e.scalar_tensor_tensor(GI.ap, GI.ap, 1.0, CV.ap, ALU.add, ALU.mult),
                 reads=[GI, CV], writes=[GI])
            S.op("act", lambda e: e.activation(MU.ap, GA.ap, AF.Square), reads=[GA], writes=[MU])
            if q + 1 < 16:
                rnn_cast(q + 1)
            S.op("act", lambda e: e.activation(MU.ap, MU.ap, AF.Sqrt, scale=-1.0, bias=1.0), reads=[MU], writes=[MU])
            if nxt_q is not None:
                rnn_conv(nxt_q)
            if own:
                S.op("dve", lambda e: e.tensor_scalar(MU.ap[:, 0:1], MU.ap[:, 0:1], pp(PP_FLAG).ap, pp(PP_OMF).ap,
                                                      ALU.mult, ALU.add),
                     reads=[MU, PP], writes=[MU])
            else:
                S.op("dve", lambda e: e.memset(MU.ap[:, 0:1], 1.0), reads=[MU], writes=[MU])
            S.op("dve", lambda e: e.scalar_tensor_tensor(GI.ap, GI.ap, 0.25 if own else 0.5, MU.ap, ALU.mult, ALU.mult),
                 reads=[GI, MU], writes=[GI])
            init = hl.ap if own else 0.0
            S.op("dve", lambda e: e.tensor_tensor_scan(MU.ap, GA.ap, GI.ap, init, ALU.mult, ALU.add),
                 reads=[GA, GI, hl], writes=[MU])
            if own:
                S.op("pool", lambda e: e.tensor_tensor(YB[0].ap[:, c, :], MU.ap, RS.ap, ALU.mult),
                     reads=[MU, RS], writes=[YB[0]])
            else:
                S.op("dve", lambda e: e.tensor_scalar(hl.ap, MU.ap[:, T - 1:T], pp(PP_FLAG).ap, 0.5, ALU.mult, ALU.mult),
                     reads=[MU, PP], writes=[hl])

        def ws_base(k):
            return WS_LO + (k % 3) * 16 * K1

        rnn_tasks = []
        for own in (False, True):
            for g in range(4):
                specs = [w_in[:, C_XR + g * 256:C_XR + (g + 1) * 256]]
                if own:
                    specs.append(w_in[:, C_GRG + g * 256:C_GRG + (g + 1) * 256])
                rnn_tasks.append((own, g, specs))

        early_hooks = []

        def rnn_comp(parts, nxt, ti):
            if ti == 5:
                for hk in early_hooks:
                    hk()
            if ti == 0:
                for q in (0, 1):
                    rnn_front_proj(q, parts[0], q * 128)
                    rnn_conv(q)
                rnn_cast(0)
            for q in (2 * ti, 2 * ti + 1):
                nq = q + 2 if q + 2 < 16 else None
                rnn_step(q, parts[-1], (q % 2) * 128, nq, (nxt[0] if nq is not None else None), (q % 2) * 128)
            if ti == 7:
                dump("y_rg", V(YB[0].ap.rearrange("p a b -> p (a b)"), "sb", YB[0].lo, YB[0].hi))

        for ti, (own, g, specs) in enumerate(rnn_tasks):
            tasks.append((ws_base, make_load(specs), (lambda parts, nxt, ti=ti: rnn_comp(parts, nxt, ti))))

        o = SCR
        BIAST = sb(o, F32, [128, 16, 256]); o += 16 * K1
        QZs = [[sb(o + 4 * K1 * (2 * i + hh_), BF16, [128, 2, T]) for hh_ in range(2)] for i in range(2)]; o += 16 * K1
        QZh = [sb(YB[1].lo + 8 * K1 + 4 * K1 * hh_, BF16, [128, 2, T]) for hh_ in range(2)]
        QZb = [QZh, QZs[0], QZs[1], QZs[0]]
        SGs = [sb(o, F32, [128, 2, T]), sb(YB[2].lo, F32, [128, 2, T])]; o += 8 * K1
        EIN = [sb(o + 2 * K1 * i, F32, [128, 512]) for i in range(4)]; o += 8 * K1
        MASKC = sb(EIN[0].lo, F32, [128, 256])
        EX = [sb(o + K1 * i, BF16, [128, 2, 2, 128]) for i in range(4)]; o += 4 * K1
        RD2 = [sb(o + K1 * i, F32, [128, 256]) for i in range(2)]; o += 2 * K1
        TN2 = [sb(o + K1 * i, F32, [128, 256]) for i in range(2)]; o += 2 * K1
        assert o <= SCR_END, o
        LB = [bank(2 + i) for i in range(4)]
        ND = [bank(6), bank(7)]
        lb_ctr = [0]

        def swa_setup():
            S.op("dve", lambda e: e.tensor_scalar(der(DR_NEGF).ap, pp(PP_FLAG).ap, -1.0, -NEG, ALU.add, ALU.mult),
                 reads=[PP], writes=[der(DR_NEGF)])
            swa_zero([1, 2])
            S.op("sp", lambda e: e.dma_start(BIAST.ap.rearrange("p a b -> p (a b)"), biasg_d), writes=[BIAST], dma=True)
            S.op("sp", lambda e: e.dma_start(MASKC.ap, maskc_d), writes=[MASKC], dma=True)
            S.op("dve", lambda e: e.tensor_tensor(BIAST.ap, BIAST.ap, MASKC.ap.unsqueeze(1).to_broadcast([128, 16, 256]),
                                                  ALU.add),
                 reads=[BIAST, MASKC], writes=[BIAST])

        def swa_zero(idx):
            for i in idx:
                for hh_ in range(2):
                    qz = QZb[i][hh_]
                    S.op("dve", lambda e, qz=qz: e.memset(qz.ap.rearrange("p a b -> p (a b)"), 0.0), writes=[qz])

        early_hooks.append(lambda: swa_zero([0]))

        def swa_proj_gen(parts, g):
            QZ = QZb[g]
            SG = SGs[(g + 1) % 2]
            for j in range(4):
                for nt in range(2):
                    bk = proj(parts[j // 2], (j % 2) * 128, HT_OWN, nt * 512, 512)
                    if j < 2:
                        for hh_ in range(2):
                            S.op("act", lambda e, bk=bk, j=j, nt=nt, hh_=hh_: e.copy(
                                QZ[hh_].ap[hh_ * 64:(hh_ + 1) * 64, j, nt * 512:(nt + 1) * 512],
                                bk.ap[hh_ * 64:(hh_ + 1) * 64, :]),
                                reads=[bk], writes=[QZ[hh_]])
                    else:
                        sgv = SG.ap[:, j - 2, nt * 512:(nt + 1) * 512]
                        S.op("act", lambda e, bk=bk, sgv=sgv: e.activation(sgv, bk.ap, AF.Exp, scale=-1.0),
                             reads=[bk], writes=[SG])
                        S.op("act", lambda e, sgv=sgv: e.activation(sgv, sgv, AF.Ln, bias=1.0), reads=[SG], writes=[SG])
                        S.op("act", lambda e, sgv=sgv: e.activation(sgv, sgv, AF.Exp, scale=-1.0), reads=[SG], writes=[SG])
                        S.op("dve", lambda e, bk=bk, sgv=sgv: e.scalar_tensor_tensor(sgv, sgv, 2.0, bk.ap, ALU.mult, ALU.mult),
                             reads=[SG, bk], writes=[SG])
                    yield

        def swa_attn(g, gen):
            QZ = QZb[g]
            SG = SGs[(g + 1) % 2]
            blocks = [(cl, i) for cl in range(2) for i in range(8)]
            npair = len(blocks) // 2

            def qk(b):
                cl, i = blocks[b]
                c = 2 * g + cl
                kvh = c // 4
                lb = LB[b % 4]
                for hh in range(2):
                    for j in range(2):
                        mm(lb.ap[:, (hh * 2 + j) * 128:(hh * 2 + j + 1) * 128],
                           KTD.ap[:, kvh, (i + j) * 128:(i + j + 1) * 128],
                           QZ[hh].ap[:, cl, i * 128:(i + 1) * 128],
                           True, True, [KTD, QZ[hh]], [lb])

            def bias(b):
                cl, i = blocks[b]
                c = 2 * g + cl
                lb, ein = LB[b % 4], EIN[b % 4]
                S.op("dve", lambda e: e.scalar_tensor_tensor(
                    ein.ap, lb.ap, 0.125, BIAST.ap[:, 2 * c:2 * c + 2, :].rearrange("p a b -> p (a b)"),
                    ALU.mult, ALU.add),
                    reads=[lb, BIAST], writes=[ein])
                if i == 0:
                    v4 = ein.ap.rearrange("p (a b c) -> p a b c", a=2, b=2)[:, :, 0, :]
                    S.op("dve", lambda e: e.tensor_scalar(v4, v4, der(DR_NEGF).ap, None, ALU.add),
                         reads=[ein, der(DR_NEGF)], writes=[ein])

            def expo(b):
                ein, ex = EIN[b % 4], EX[b % 4]
                S.op("act", lambda e: e.activation(ex.ap.rearrange("p a b c -> p (a b c)"), ein.ap, AF.Exp),
                     reads=[ein], writes=[ex])

            def pv(b):
                cl, i = blocks[b]
                c = 2 * g + cl
                kvh = c // 4
                ex = EX[b % 4]
                ndb = ND[(b // 2) % 2]
                s_ = b % 2
                n_ = 0
                for hh in range(2):
                    for j in range(2):
                        mm(ndb.ap[:, s_ * 128:(s_ + 1) * 128], VV.ap[:, i + j, kvh, hh, :], ex.ap[:, hh, j, :],
                           n_ == 0, n_ == 3, [VV, ex], [ndb])
                        n_ += 1
                n_ = 0
                for hh in range(2):
                    for j in range(2):
                        mm(ndb.ap[:, 256 + s_ * 128:256 + (s_ + 1) * 128], (OZ if hh == 0 else ZO).ap, ex.ap[:, hh, j, :],
                           n_ == 0, n_ == 3, [OZ, ZO, ex], [ndb])
                        n_ += 1

            def n_act(p):
                cl, i = blocks[2 * p]
                esk = der(DR_ESINK + 2 * g + cl)
                ndb, rd = ND[p % 2], RD2[p % 2]
                S.op("act", lambda e: e.activation(rd.ap, ndb.ap[:, 256:512], AF.Ln, bias=esk.ap),
                     reads=[ndb, esk], writes=[rd])
                S.op("act", lambda e: e.activation(rd.ap, rd.ap, AF.Exp, scale=-1.0), reads=[rd], writes=[rd])

            def n_dve(p):
                cl, i = blocks[2 * p]
                c = 2 * g + cl
                ndb, rd, tn = ND[p % 2], RD2[p % 2], TN2[p % 2]
                t0 = i * 128
                S.op("dve", lambda e: e.tensor_tensor(tn.ap, ndb.ap[:, 0:256], rd.ap, ALU.mult),
                     reads=[ndb, rd], writes=[tn])
                ych = V(None, "sb", YB[1].lo + c * 2 * K1, YB[1].lo + (c + 1) * 2 * K1)
                S.op("dve", lambda e: e.scalar_tensor_tensor(
                    YB[1].ap[:, c, t0:t0 + 256], tn.ap, 0.5, SG.ap[:, cl, t0:t0 + 256], ALU.mult, ALU.mult),
                    reads=[tn, SG], writes=[ych])

            qk(0); qk(1); bias(0); bias(1)
            ngen = 0
            for k in range(npair + 2):
                if k + 1 < npair:
                    qk(2 * k + 2); qk(2 * k + 3)
                    bias(2 * k + 2); bias(2 * k + 3)
                if 0 <= k - 2 < npair:
                    n_dve(k - 2)
                if k < npair:
                    expo(2 * k); expo(2 * k + 1)
                if 0 <= k - 1 < npair:
                    n_act(k - 1)
                if gen is not None and ngen < 8 and k < 8:
                    ngen += 1
                    next(gen, None)
                if k < npair:
                    pv(2 * k); pv(2 * k + 1)

        swa_gen = [None]

        swa_hooks = {}

        def swa_comp(parts, nxt, g):
            if g == 0:
                swa_setup()
                for _ in swa_proj_gen(parts, 0):
                    pass
            gen = swa_proj_gen(nxt, g + 1) if g < 3 else swa_hooks["g3_gen"]()
            swa_attn(g, gen)
            if gen is not None:
                for _ in gen:
                    pass
            if g == 2:
                swa_hooks["after_g2"]()
            if g == 3:
                dump("y_swa", V(YB[1].ap.rearrange("p a b -> p (a b)"), "sb", YB[1].lo, YB[1].hi))

        for g in range(4):
            specs = [w_in[:, C_QS + g * 256:C_QS + (g + 1) * 256], w_in[:, C_GSWA + g * 256:C_GSWA + (g + 1) * 256]]
            tasks.append((ws_base, make_load(specs), (lambda parts, nxt, g=g: swa_comp(parts, nxt, g))))

        o = SCR
        MEMT = sb(YB[2].lo, BF16, [128, KC, 256])
        MXN = [sb(YB[2].lo + 8 * K1, BF16, [128, D])]
        MJ = sb(YB[2].lo + 12 * K1, BF16, [128, D])
        MXS = [sb(QZs[1][0].lo, F32, [128, D])]
        MKT = sb(o, BF16, [128, 8, 256]); o += 4 * K1
        MV = sb(o, BF16, [128, 2, 1024]); o += 4 * K1
        o2 = o
        mn1, mn2 = norm_stages(memx, PP_MEMG, MEMT, MXS, MXN, MJ, [bank_bf(0, 2)], DR_SS + 9)

        def mem_norm_gen():
            yield
            yield
            mn1(0, with_load=False); mn1.load(1)
            yield
            yield
            mn2(0)
            yield
            mn1(1, with_load=False)
            yield
            yield
            mn2(1)
            yield
        swa_hooks["after_g2"] = lambda: mn1.load(0)
        swa_hooks["g3_gen"] = mem_norm_gen
        QMb = [sb(o2 + 4 * K1 * i, BF16, [128, 2, T]) for i in range(2)]; o2 += 8 * K1
        SGMb = [sb(o2 + 8 * K1 * i, F32, [128, 2, T]) for i in range(2)]; o2 += 16 * K1
        EXM = [sb(o2 + 2 * K1 * i, BF16, [128, 2, 512]) for i in range(2)]; o2 += 4 * K1
        RDMb = [sb(o2 + 2 * K1 * i, F32, [128, 512]) for i in range(2)]; o2 += 4 * K1
        TNM = [sb(o2 + 2 * K1 * i, F32, [128, 512]) for i in range(2)]; o2 += 4 * K1
        assert o2 <= SCR_END, o2
        LMB = [bank(2), bank(3)]
        NMB = [bank(4), bank(5)]
        DMB = bank(6)
        ex_ctr = [0]

        def mkv_group(parts, gg):
            part = parts[0]
            if gg < 2:
                for j in range(4):
                    bk = proj(part, j * 128, MEMT, 0, 256)
                    S.op("act", lambda e, bk=bk, j=j: e.copy(MKT.ap[:, gg * 4 + j, :], bk.ap[:, 0:256]),
                         reads=[bk], writes=[MKT])
            else:
                for mc in range(2):
                    bk = bank(pj_ctr[0] % 2)
                    pj_ctr[0] += 1
                    for kc in range(KC):
                        mm(bk.ap, MEMT.ap[:, kc, mc * 128:(mc + 1) * 128], part.ap[:, kc, :], kc == 0, kc == KC - 1,
                           [MEMT, part], [bk])
                    S.op("act", lambda e, bk=bk, mc=mc: e.copy(MV.ap[:, mc, (gg - 2) * 512:(gg - 1) * 512], bk.ap),
                         reads=[bk], writes=[MV])

        def mem_proj_gen(parts, m):
            QM, SGM = QMb[m % 2], SGMb[m % 2]
            for j in range(4):
                for nt in range(2):
                    bk = proj(parts[j // 2], (j % 2) * 128, HT_OWN, nt * 512, 512)
                    if j < 2:
                        S.op("act", lambda e, bk=bk, j=j, nt=nt: e.copy(QM.ap[:, j, nt * 512:(nt + 1) * 512], bk.ap),
                             reads=[bk], writes=[QM])
                    else:
                        sgv = SGM.ap[:, j - 2, nt * 512:(nt + 1) * 512]
                        S.op("act", lambda e, bk=bk, sgv=sgv: e.activation(sgv, bk.ap, AF.Exp, scale=-1.0),
                             reads=[bk], writes=[SGM])
                        S.op("act", lambda e, sgv=sgv: e.activation(sgv, sgv, AF.Ln, bias=1.0), reads=[SGM], writes=[SGM])
                        S.op("act", lambda e, sgv=sgv: e.activation(sgv, sgv, AF.Exp, scale=-1.0), reads=[SGM], writes=[SGM])
                        S.op("dve", lambda e, bk=bk, sgv=sgv: e.scalar_tensor_tensor(sgv, sgv, 2.0, bk.ap, ALU.mult, ALU.mult),
                             reads=[SGM, bk], writes=[SGM])
                    yield

        def mem_group(parts, nxt, m):
            QM, SGM = QMb[m % 2], SGMb[m % 2]
            if m == 0:
                for _ in mem_proj_gen(parts, 0):
                    pass
            gen = mem_proj_gen(nxt, m + 1) if m < 3 else None

            def adv(n):
                if gen is not None:
                    for _ in range(n):
                        next(gen, None)

            exms = [EXM[0], EXM[1]]

            def qk(nt):
                for mc in range(2):
                    lm = LMB[mc]
                    for dc in range(2):
                        mm(lm.ap, MKT.ap[:, 2 * m + dc, mc * 128:(mc + 1) * 128], QM.ap[:, dc, nt * 512:(nt + 1) * 512],
                           dc == 0, dc == 1, [MKT, QM], [lm])

            def expo(nt):
                exm = exms[nt]
                for mc in range(2):
                    lm = LMB[mc]
                    S.op("act", lambda e, lm=lm, mc=mc: e.activation(exm.ap[:, mc, :], lm.ap, AF.Exp, scale=1.0 / 16),
                         reads=[lm], writes=[exm])

            def pv(nt):
                exm = exms[nt]
                for dc in range(2):
                    for mc in range(2):
                        mm(NMB[dc].ap, MV.ap[:, mc, (2 * m + dc) * 128:(2 * m + dc + 1) * 128], exm.ap[:, mc, :],
                           mc == 0, mc == 1, [MV, exm], [NMB[dc]])
                for mc in range(2):
                    mm(DMB.ap, ONES.ap, exm.ap[:, mc, :], mc == 0, mc == 1, [ONES, exm], [DMB])

            def n_act(nt):
                RDM = RDMb[nt]
                S.op("act", lambda e: e.activation(RDM.ap, DMB.ap, AF.Ln), reads=[DMB], writes=[RDM])
                S.op("act", lambda e: e.activation(RDM.ap, RDM.ap, AF.Exp, scale=-1.0), reads=[RDM], writes=[RDM])

            def n_dve(nt):
                for dc in range(2):
                    tn = TNM[dc]
                    S.op("dve", lambda e, dc=dc, tn=tn, RDM=RDMb[nt]: e.tensor_tensor(tn.ap, NMB[dc].ap, RDM.ap, ALU.mult),
                         reads=[NMB[dc], RDMb[nt]], writes=[tn])
                    S.op("dve", lambda e, dc=dc, tn=tn: e.scalar_tensor_tensor(
                        YB[2].ap[:, 2 * m + dc, nt * 512:(nt + 1) * 512], tn.ap, 0.5,
                        SGM.ap[:, dc, nt * 512:(nt + 1) * 512], ALU.mult, ALU.mult),
                        reads=[tn, SGM], writes=[YB[2]])

            qk(0); adv(2); expo(0); pv(0)
            qk(1); adv(2); expo(1); n_act(0); n_dve(0)
            adv(2); pv(1); adv(2); n_act(1); n_dve(1)
            if gen is not None:
                for _ in gen:
                    pass
            if m == 3:
                dump("y_mem", V(YB[2].ap.rearrange("p a b -> p (a b)"), "sb", YB[2].lo, YB[2].hi))

        for gg in range(4):
            tasks.append((ws_base, make_load([w_mkv[:, gg * 512:(gg + 1) * 512]]), (lambda parts, nxt, gg=gg: mkv_group(parts, gg))))
        for m in range(4):
            specs = [w_in[:, C_QM + m * 256:C_QM + (m + 1) * 256], w_in[:, C_GMEM + m * 256:C_GMEM + (m + 1) * 256]]
            tasks.append((ws_base, make_load(specs), (lambda parts, nxt, m=m: mem_group(parts, nxt, m))))

        MERGED = sb(ARENA_BYTES - 32 * K1, BF16, [128, KC, T])
        o = 156 * K1
        SIG = [sb(o + 2 * K1 * i, F32, [128, 512]) for i in range(3)]; o += 6 * K1
        ACC = [sb(o + 2 * K1 * i, F32, [128, 512]) for i in range(2)]; o += 4 * K1
        TT = [sb(o + 2 * K1 * i, F32, [128, 512]) for i in range(2)]; o += 4 * K1
        assert o <= MERGED.lo
        WO = [sb(140 * K1, BF16, [128, KC, 512]), sb(6 * K1, BF16, [128, KC, 512]),
              sb(22 * K1, BF16, [128, KC, 512]), sb(102 * K1, BF16, [128, KC, 512])]
        bk_ctr = [0]
        sg_ctr = [0]

        def s5_base(k):
            return WS_LO + (k % 3) * 18 * K1

        def s5_compute(parts, dc):
            G5 = parts[0:3]
            B5 = parts[3:6]
            if dc == 1:
                wload(WO[0], w_out[:, 0:512])
            for nt in range(2):
                acc = ACC[(dc * 2 + nt) % 2]
                for i in range(3):
                    ba = bank(bk_ctr[0] % 8)
                    bb = bank((bk_ctr[0] + 1) % 8)
                    bk_ctr[0] += 2
                    sg = SIG[sg_ctr[0] % 3]
                    tt_ = TT[sg_ctr[0] % 2]
                    sg_ctr[0] += 1
                    for kc in range(KC):
                        mm(ba.ap, G5[i].ap[:, kc, :], HT_OWN.ap[:, kc, nt * 512:(nt + 1) * 512], kc == 0, kc == KC - 1,
                           [G5[i], HT_OWN], [ba])
                    for kc in range(8):
                        mm(bb.ap, B5[i].ap[:, kc, :], YB[i].ap[:, kc, nt * 512:(nt + 1) * 512], kc == 0, kc == 7,
                           [B5[i], YB[i]], [bb])
                    S.op("act", lambda e, ba=ba, sg=sg: e.activation(sg.ap, ba.ap, AF.Sigmoid), reads=[ba], writes=[sg])
                    if i == 0:
                        S.op("dve", lambda e, sg=sg, bb=bb, acc=acc: e.tensor_tensor(acc.ap, sg.ap, bb.ap, ALU.mult),
                             reads=[sg, bb], writes=[acc])
                    else:
                        S.op("dve", lambda e, sg=sg, bb=bb, tt_=tt_: e.tensor_tensor(tt_.ap, sg.ap, bb.ap, ALU.mult),
                             reads=[sg, bb], writes=[tt_])
                        if i == 1:
                            S.op("dve", lambda e, acc=acc, tt_=tt_: e.tensor_tensor(acc.ap, acc.ap, tt_.ap, ALU.add),
                                 reads=[acc, tt_], writes=[acc])
                        else:
                            S.op("dve", lambda e, acc=acc, tt_=tt_, nt=nt: e.tensor_tensor(
                                MERGED.ap[:, dc, nt * 512:(nt + 1) * 512], acc.ap, tt_.ap, ALU.add),
                                reads=[acc, tt_], writes=[MERGED])

        for dc in range(16):
            specs = [w_in[:, C_GATE + i * 2048 + dc * 128:C_GATE + i * 2048 + (dc + 1) * 128] for i in range(3)]
            specs += [w_br[i, :, dc * 128:(dc + 1) * 128] for i in range(3)]
            if dc == 0:
                offs0 = [KTD.lo + 4 * K1 * i for i in range(3)] + [SCR + 50 * K1 + 2 * K1 * i for i in range(3)]
                tasks.append((s5_base, make_load(specs, offs0), (lambda parts, nxt, dc=dc: s5_compute(parts, dc)), True))
            else:
                tasks.append((s5_base, make_load(specs), (lambda parts, nxt, dc=dc: s5_compute(parts, dc))))

        loaded = []
        ring_k = {}
        for i in range(len(tasks)):
            while len(loaded) <= i or (len(loaded) < min(len(tasks), i + 3) and
                                       (tasks[len(loaded)][0] is tasks[i][0] or len(tasks[len(loaded)]) > 3)):
                k = len(loaded)
                basefn, load = tasks[k][0], tasks[k][1]
                if len(tasks[k]) > 3:
                    loaded.append(load(0))
                    continue
                rk = ring_k.get(basefn, 0)
                ring_k[basefn] = rk + 1
                loaded.append(load(basefn(rk)))
            tasks[i][2](loaded[i], loaded[i + 1] if i + 1 < len(loaded) else None)
        dump("merged", V(MERGED.ap.rearrange("p a b -> p (a b)"), "sb", MERGED.lo, MERGED.hi))

        for cg in range(1, 4):
            wload(WO[cg], w_out[:, cg * 512:(cg + 1) * 512])
        OUTU = [sb((38 + 8 * i) * K1, F32, [128, D]) for i in range(8)]
        POSTG = sb(118 * K1, F32, [128, D])
        XRS = [sb(126 * K1, F32, [128, D]), sb(156 * K1, F32, [128, D])]
        OJ = sb(134 * K1, BF16, [128, D])
        S.op("sp", lambda e: e.dma_start(POSTG.ap, postg_d), writes=[POSTG], dma=True)
        SQP = sb(138 * K1, F32, [128, 32])

        def s6_mm(cg, tt):
            bk = bank(bk_ctr[0] % 8)
            bk_ctr[0] += 1
            for kc in range(KC):
                mm(bk.ap, MERGED.ap[:, kc, tt * 128:(tt + 1) * 128], WO[cg].ap[:, kc, :],
                   kc == 0, kc == KC - 1, [MERGED, WO[cg]], [bk])
            S.op("act", lambda e: e.copy(OUTU[tt].ap[:, cg * 512:(cg + 1) * 512], bk.ap),
                 reads=[bk], writes=[OUTU[tt]])
            c_ = 4 * tt + cg
            sqv = V(SQP.ap[:, c_:c_ + 1], "sb", SQP.lo + 4 * c_, SQP.lo + 4 * c_ + 4)
            S.op("act", lambda e: e.activation(OJ.ap[:, cg * 512:(cg + 1) * 512], bk.ap, AF.Square, accum_out=sqv.ap),
                 reads=[bk], writes=[OJ, sqv])

        for cg in range(2):
            for tt in range(8):
                s6_mm(cg, tt)
        for tt in range(8):
            ou, xrs = OUTU[tt], XRS[tt % 2]
            ssc = 112 + 3 * (tt % 2)
            s6_mm(2, tt)
            s6_mm(3, tt)
            sq4 = V(SQP.ap[:, 4 * tt:4 * tt + 4], "sb", SQP.lo + 16 * tt, SQP.lo + 16 * tt + 16)
            S.op("dve", lambda e, sq4=sq4, ssc=ssc: e.reduce_sum(der(ssc).ap, sq4.ap, mybir.AxisListType.X),
                 reads=[sq4], writes=[der(ssc)])
            S.op("act", lambda e, ssc=ssc: e.activation(der(ssc + 1).ap, der(ssc).ap, AF.Ln, scale=1.0 / D, bias=EPS),
                 reads=[der(ssc)], writes=[der(ssc + 1)])
            S.op("act", lambda e, ssc=ssc: e.activation(der(ssc + 2).ap, der(ssc + 1).ap, AF.Exp, scale=-0.5),
                 reads=[der(ssc + 1)], writes=[der(ssc + 2)])
            if tt == 0:
                S.op("sp", lambda e: e.dma_start(XRS[0].ap, xo[7 * 128:8 * 128, :]), writes=[XRS[0]], dma=True)
            npiece = 2 if tt == 7 else 1
            w_ = D // npiece
            for pc in range(npiece):
                ouv = V(ou.ap[:, pc * w_:(pc + 1) * w_], "sb", ou.lo + 4 * pc * w_, ou.lo + 4 * (pc + 1) * w_)
                pgv = POSTG.ap[:, pc * w_:(pc + 1) * w_]
                S.op("dve", lambda e, ouv=ouv, pgv=pgv, ssc=ssc: e.scalar_tensor_tensor(
                    ouv.ap, ouv.ap, der(ssc + 2).ap, pgv, ALU.mult, ALU.mult),
                    reads=[ouv, der(ssc + 2), POSTG], writes=[ouv])
                if tt == 7:
                    xrv = XRS[0].ap[:, pc * w_:(pc + 1) * w_]
                    S.op("dve", lambda e, ouv=ouv, xrv=xrv: e.tensor_tensor(ouv.ap, ouv.ap, xrv, ALU.add),
                         reads=[ouv, XRS[0]], writes=[ouv])
                else:
                    S.op("pool", lambda e, ouv=ouv, tt=tt, pc=pc, w_=w_: e.dma_start(
                        ouv.ap, xo[tt * 128:(tt + 1) * 128, pc * w_:(pc + 1) * w_], accum_op=ALU.add),
                        reads=[ouv], writes=[ouv], dma=True)
                S.op("sp", lambda e, ouv=ouv, tt=tt, pc=pc, w_=w_: e.dma_start(
                    out_d[tt * 128:(tt + 1) * 128, pc * w_:(pc + 1) * w_], ouv.ap),
                    reads=[ouv], dma=True, final=True)

        S.emit(nc)
    return nc


def _rel_bucket_np(dist):
    n = np.maximum(dist, 0)
    max_exact = 16
    ratio = np.log(np.maximum(n, 1).astype(np.float32) / np.float32(max_exact)) / np.float32(np.log(128 / max_exact))
    large = max_exact + (ratio.astype(np.float32) * np.float32(32 - max_exact)).astype(np.int32)
    large = np.minimum(large, 31)
    return np.where(n < max_exact, n, large)


def _fm(v):
    return np.ascontiguousarray(v.reshape(-1, 128).T)


def make_in_maps(x, mem, pre_norm_g, post_norm_g, mem_norm_g, w_in, conv_w, conv_b, w_rg_a, b_rg_a,
                 w_rg_x, b_rg_x, lru_lambda, swa_sinks, rel_bias, w_mem_kv, w_br_rg, w_br_swa,
                 w_br_mem, w_out):
    f = np.float32
    x = np.asarray(x, f)
    mem = np.asarray(mem, f)
    w_in0 = np.ascontiguousarray(np.asarray(w_in, f)[0])
    w_mkv0 = np.ascontiguousarray(np.asarray(w_mem_kv, f)[0])
    w_br0 = np.ascontiguousarray(np.stack([np.asarray(w_br_rg, f)[0], np.asarray(w_br_swa, f)[0],
                                           np.asarray(w_br_mem, f)[0]]))
    w_out0 = np.ascontiguousarray(np.asarray(w_out, f)[0])
    wrg = np.stack([np.asarray(w_rg_a, f)[0], np.asarray(w_rg_x, f)[0]])
    wrg = np.ascontiguousarray(wrg.transpose(2, 0, 1, 3).reshape(128, 2 * 8 * 128))
    k = np.arange(128)[:, None, None]
    j = np.arange(2)[None, :, None]
    q = np.arange(128)[None, None, :]
    dist = q + 128 - (j * 128 + k)
    valid = (dist >= 0) & (dist < 128)
    bucket = _rel_bucket_np(dist)
    rb = np.asarray(rel_bias, f)
    biasg = rb[bucket]
    biasg = np.ascontiguousarray(biasg.transpose(0, 3, 1, 2).reshape(128, 16 * 256))
    maskc = np.ascontiguousarray(np.where(valid, f(0.0), f(NEG)).astype(f).reshape(128, 256))
    ident = np.eye(128, dtype=f)
    postg = np.ascontiguousarray(np.broadcast_to(np.asarray(post_norm_g, f)[0][None, :], (128, D)))
    sinks = np.asarray(swa_sinks, f)[0]
    sinkc = np.empty((128, 8), f)
    for c in range(8):
        sinkc[0:64, c] = sinks[2 * c]
        sinkc[64:128, c] = sinks[2 * c + 1]
    cw = np.asarray(conv_w, f)[0]
    in_maps = []
    for core in range(8):
        b, half = core // 2, core % 2
        pp = np.zeros((128, NPP), f)
        pp[:, PP_PREG:PP_PREG + 16] = _fm(np.asarray(pre_norm_g, f)[0])
        pp[:, PP_MEMG:PP_MEMG + 16] = _fm(np.asarray(mem_norm_g, f)[0])
        for kk in range(4):
            pp[:, PP_CW + kk:PP_CW + 32:4] = _fm(cw[kk])
        pp[:, PP_CB:PP_CB + 8] = _fm(np.asarray(conv_b, f)[0])
        pp[:, PP_BA:PP_BA + 8] = _fm(np.asarray(b_rg_a, f)[0])
        pp[:, PP_BX:PP_BX + 8] = _fm(np.asarray(b_rg_x, f)[0])
        pp[:, PP_LAM:PP_LAM + 8] = _fm(np.asarray(lru_lambda, f)[0])
        pp[:, PP_SINK:PP_SINK + 8] = sinkc
        pp[:, PP_FLAG] = f(half)
        pp[:, PP_OMF] = f(1 - half)
        xo = np.ascontiguousarray(x[b, half * T:(half + 1) * T])
        xpre = np.ascontiguousarray(x[b, 0:T]) if half == 1 else np.zeros((T, D), f)
        in_maps.append({
            "xo": xo, "xp": xpre, "memx": np.ascontiguousarray(mem[b]),
            "w_in": w_in0, "w_mkv": w_mkv0, "w_br": w_br0, "w_out": w_out0, "w_rg": wrg,
            "pp": pp, "postg": postg, "biasg": biasg, "maskc": maskc, "ident": ident,
        })
    return in_maps


_DBG = []


def kernel(**inputs):
    in_maps = make_in_maps(**inputs)
    nc = build_program(_DBG)
    res = run_bass_kernel_spmd(nc, in_maps, core_ids=list(range(8)))
    out = np.empty((4, 2048, D), np.float32)
    for core in range(8):
        b, half = core // 2, core % 2
        out[b, half * T:(half + 1) * T] = res.results[core]["out"]
    if _DBG:
        kernel.last_results = res.results
    return out
```
